# Optimizing a Trainium2 kernel written in Bass

```python
import math
import jax, jax.numpy as jnp
from jax import lax
import numpy as np

D_MODEL = 2048
BATCH = 8
SEQ = 2048
DEPTH = 2
DEC_BATCH = 8
DEC_SEQ = 4096
PAST_LEN = 128

HEAD_DIM = 128
GRID_W = 64
Q_BLOCK = 128
EPS = 1e-6
NEG_INF = -1e30

A_HEADS = D_MODEL // (4 * HEAD_DIM)
B_HEADS = D_MODEL // (2 * HEAD_DIM)
NA_KH_MAX = 8
NA_KW = 16
C_PATTERNS = ((128, 1), (512, 4), (2048, 16))
C_HEADS_PER_GROUP = D_MODEL // (4 * HEAD_DIM)
C_HEADS = len(C_PATTERNS) * C_HEADS_PER_GROUP
D_HEADS = 3 * D_MODEL // (4 * HEAD_DIM)
D_KV_HEADS = D_HEADS // 3
ROPE_THETA = 10000.0
ROPE_ROW_DIMS = HEAD_DIM // 2
ROPE_COL_DIMS = HEAD_DIM - ROPE_ROW_DIMS
FFN_HIDDEN = ((8 * D_MODEL + 3 * 256 - 1) // (3 * 256)) * 256

A_W = A_HEADS * 2 * HEAD_DIM
B_W = B_HEADS * HEAD_DIM
EV_IN = 3 * A_W + 3 * B_W
EV_OUT = A_W + B_W
C_W = C_HEADS * HEAD_DIM
DQ_W = D_HEADS * HEAD_DIM
DKV_W = D_KV_HEADS * HEAD_DIM
OD_IN = 3 * C_W + DQ_W + 2 * DKV_W
OD_OUT = C_HEADS_PER_GROUP * HEAD_DIM + DQ_W
N_EVEN = (DEPTH + 1) // 2
N_ODD = DEPTH // 2

kernel_name = "hybrid_diff_na_dilated_axial_encoder"


def rms_norm(x, g):
    xf = x.astype(jnp.float32)
    y = xf * lax.rsqrt(jnp.mean(xf * xf, axis=-1, keepdims=True) + EPS)
    return (y * g.astype(jnp.float32)).astype(x.dtype)


def alibi_slopes(n):
    return jnp.asarray([2.0 ** (-8.0 * (i + 1) / n) for i in range(n)], dtype=jnp.float32)


def lambda_init_fn(layer_idx):
    return 0.8 - 0.6 * math.exp(-0.3 * layer_idx)


def diff_attention(q, k, v, lam, slopes):
    B, H, S, _, hd = q.shape
    nblk = S // Q_BLOCK
    scale = hd ** -0.5
    kpos = jnp.arange(S)
    qblocks = q.reshape(B, H, nblk, Q_BLOCK, 2, hd).transpose(2, 0, 1, 3, 4, 5)

    def block(args):
        qi, bi = args
        s = jnp.einsum("bhqcd,bhkcd->bhcqk", qi, k, preferred_element_type=jnp.float32) * scale
        qpos = bi * Q_BLOCK + jnp.arange(Q_BLOCK)
        dist = jnp.abs(qpos[:, None] - kpos[None, :]).astype(jnp.float32)
        s = s - (slopes[:, None, None] * dist)[None, :, None]
        p = jax.nn.softmax(s, axis=-1)
        a = p[:, :, 0] - lam * p[:, :, 1]
        return jnp.einsum("bhqk,bhkd->bhqd", a.astype(v.dtype), v)

    o = lax.map(block, (qblocks, jnp.arange(nblk)))
    return o.transpose(1, 2, 0, 3, 4).reshape(B, H, S, 2 * hd)


def neighbourhood_attention(q, k, v, rpb):
    B, H, S, hd = q.shape
    rows = S // GRID_W
    kh = min(NA_KH_MAX, rows)
    scale = hd ** -0.5
    qg = q.reshape(B, H, rows, GRID_W, hd)
    kg = k.reshape(B, H, rows, GRID_W, hd)
    vg = v.reshape(B, H, rows, GRID_W, hd)
    c = jnp.arange(GRID_W)
    c0 = jnp.clip(c - NA_KW // 2, 0, GRID_W - NA_KW)
    col_mask = (c[None, :] >= c0[:, None]) & (c[None, :] < c0[:, None] + NA_KW)
    dc = jnp.clip(c[None, :] - c[:, None], 1 - NA_KW, NA_KW - 1) + NA_KW - 1
    rpb_cols = jnp.take(rpb, dc, axis=2)

    def row_block(r):
        r0 = jnp.clip(r - kh // 2, 0, rows - kh)
        qr = lax.dynamic_index_in_dim(qg, r, axis=2, keepdims=False)
        kr = lax.dynamic_slice_in_dim(kg, r0, kh, axis=2)
        vr = lax.dynamic_slice_in_dim(vg, r0, kh, axis=2)
        s = jnp.einsum("bhqd,bhiwd->bhqiw", qr, kr, preferred_element_type=jnp.float32) * scale
        dr = r0 + jnp.arange(kh) - r + NA_KH_MAX - 1
        bias = jnp.take(rpb_cols, dr, axis=1).transpose(0, 2, 1, 3)
        s = jnp.where(col_mask[:, None, :], s + bias[None], NEG_INF)
        p = jax.nn.softmax(s.reshape(B, H, GRID_W, kh * GRID_W), axis=-1).reshape(s.shape)
        return jnp.einsum("bhqiw,bhiwd->bhqd", p.astype(v.dtype), vr)

    o = lax.map(row_block, jnp.arange(rows))
    return o.transpose(1, 2, 0, 3, 4).reshape(B, H, S, hd)


def dilated_group(q, k, v, dil, radius, slopes):
    B, H, S, hd = q.shape
    L = S // dil
    scale = hd ** -0.5

    def split(a):
        return a.reshape(B, H, L, dil, hd).transpose(0, 1, 3, 2, 4)

    qs, ks, vs = split(q), split(k), split(v)
    qb = min(Q_BLOCK, L)
    nblk = -(-L // qb)
    Lp = nblk * qb
    qs = jnp.pad(qs, ((0, 0), (0, 0), (0, 0), (0, Lp - L), (0, 0)))
    kpad = ((0, 0), (0, 0), (0, 0), (radius, radius + Lp - L), (0, 0))
    kp, vp = jnp.pad(ks, kpad), jnp.pad(vs, kpad)
    band = jnp.arange(nblk)[:, None] * qb + jnp.arange(qb + 2 * radius)[None, :]
    kb = jnp.take(kp, band, axis=3)
    vb = jnp.take(vp, band, axis=3)
    qbk = qs.reshape(B, H, dil, nblk, qb, hd)
    s = jnp.einsum("bhrnqd,bhrnkd->bhrnqk", qbk, kb, preferred_element_type=jnp.float32) * scale
    qm = jnp.arange(nblk)[:, None] * qb + jnp.arange(qb)[None, :]
    km = band - radius
    rel = km[:, None, :] - qm[:, :, None]
    valid = (jnp.abs(rel) <= radius) & (km[:, None, :] >= 0) & (km[:, None, :] < L)
    dist = (dil * jnp.abs(rel)).astype(jnp.float32)
    bias = -slopes[:, None, None, None] * dist
    s = jnp.where(valid, s + bias[None, :, None], NEG_INF)
    lse = jax.nn.logsumexp(s, axis=-1)
    p = jnp.exp(s - lse[..., None])
    o = jnp.einsum("bhrnqk,bhrnkd->bhrnqd", p.astype(v.dtype), vb)
    o = o.reshape(B, H, dil, Lp, hd)[:, :, :, :L].transpose(0, 1, 3, 2, 4).reshape(B, H, S, hd)
    lse = lse.reshape(B, H, dil, Lp)[:, :, :, :L].transpose(0, 1, 3, 2).reshape(B, H, S)
    return o, lse


def axial_rope(S):
    t = jnp.arange(S)
    row = (t // GRID_W).astype(jnp.float32)
    col = (t % GRID_W).astype(jnp.float32)
    f_row = ROPE_THETA ** (-jnp.arange(0, ROPE_ROW_DIMS, 2, dtype=jnp.float32) / ROPE_ROW_DIMS)
    f_col = ROPE_THETA ** (-jnp.arange(0, ROPE_COL_DIMS, 2, dtype=jnp.float32) / ROPE_COL_DIMS)
    ang = jnp.concatenate([row[:, None] * f_row[None, :], col[:, None] * f_col[None, :]], axis=-1)
    return jnp.cos(ang), jnp.sin(ang)


def apply_rope(x, cos, sin):
    xf = x.astype(jnp.float32).reshape(x.shape[:-1] + (x.shape[-1] // 2, 2))
    x0, x1 = xf[..., 0], xf[..., 1]
    y = jnp.stack([x0 * cos - x1 * sin, x0 * sin + x1 * cos], axis=-1)
    return y.reshape(x.shape).astype(x.dtype)


def gqa_attention(q, k, v):
    B, Hq, S, hd = q.shape
    Hkv = k.shape[1]
    G = Hq // Hkv
    nblk = S // Q_BLOCK
    scale = hd ** -0.5
    qblocks = q.reshape(B, Hkv, G, nblk, Q_BLOCK, hd).transpose(3, 0, 1, 2, 4, 5)

    def block(qi):
        s = jnp.einsum("bkgqd,bksd->bkgqs", qi, k, preferred_element_type=jnp.float32) * scale
        p = jax.nn.softmax(s, axis=-1)
        return jnp.einsum("bkgqs,bksd->bkgqd", p.astype(v.dtype), v)

    o = lax.map(block, qblocks)
    return o.transpose(1, 2, 3, 0, 4, 5).reshape(B, Hq, S, hd)


def heads(a, n):
    B, S, _ = a.shape
    return a.reshape(B, S, n, -1).transpose(0, 2, 1, 3)


def even_mixer(h, w_in, lam_vec, subln_g, rpb, w_out, lambda_init):
    B, S, _ = h.shape
    proj = h @ w_in
    cuts = np.cumsum([A_W, A_W, A_W, B_W, B_W]).tolist()
    qa, ka, va, qn, kn, vn = jnp.split(proj, cuts, axis=-1)
    qa = qa.reshape(B, S, A_HEADS, 2, HEAD_DIM).transpose(0, 2, 1, 3, 4)
    ka = ka.reshape(B, S, A_HEADS, 2, HEAD_DIM).transpose(0, 2, 1, 3, 4)
    va = heads(va, A_HEADS)
    lv = lam_vec.astype(jnp.float32)
    lam = jnp.exp(jnp.sum(lv[0] * lv[1])) - jnp.exp(jnp.sum(lv[2] * lv[3])) + lambda_init
    oa = diff_attention(qa, ka, va, lam, alibi_slopes(A_HEADS))
    oa = rms_norm(oa, subln_g) * (1.0 - lambda_init)
    oa = oa.transpose(0, 2, 1, 3).reshape(B, S, A_W)
    ob = neighbourhood_attention(heads(qn, B_HEADS), heads(kn, B_HEADS), heads(vn, B_HEADS), rpb)
    ob = ob.transpose(0, 2, 1, 3).reshape(B, S, B_W)
    return jnp.concatenate([oa, ob], axis=-1) @ w_out


def odd_mixer(h, w_in, qk_norm_g, w_out):
    B, S, _ = h.shape
    proj = h @ w_in
    cuts = np.cumsum([C_W, C_W, C_W, DQ_W, DKV_W]).tolist()
    qc, kc, vc, qd, kd, vd = jnp.split(proj, cuts, axis=-1)
    ng = len(C_PATTERNS)
    qc = qc.reshape(B, S, ng, C_HEADS_PER_GROUP, HEAD_DIM)
    kc = kc.reshape(B, S, ng, C_HEADS_PER_GROUP, HEAD_DIM)
    vc = vc.reshape(B, S, ng, C_HEADS_PER_GROUP, HEAD_DIM)
    slopes = alibi_slopes(C_HEADS).reshape(ng, C_HEADS_PER_GROUP)
    outs, lses = [], []
    for g, (win, dil) in enumerate(C_PATTERNS):
        o, l = dilated_group(qc[:, :, g].transpose(0, 2, 1, 3), kc[:, :, g].transpose(0, 2, 1, 3),
                             vc[:, :, g].transpose(0, 2, 1, 3), dil, win // (2 * dil), slopes[g])
        outs.append(o)
        lses.append(l)
    alpha = jax.nn.softmax(jnp.stack(lses, axis=0), axis=0)
    oc = jnp.sum(alpha[..., None] * jnp.stack(outs, axis=0).astype(jnp.float32), axis=0).astype(h.dtype)
    oc = oc.transpose(0, 2, 1, 3).reshape(B, S, C_HEADS_PER_GROUP * HEAD_DIM)
    cos, sin = axial_rope(S)
    qd = apply_rope(rms_norm(heads(qd, D_HEADS), qk_norm_g[0]), cos, sin)
    kd = apply_rope(rms_norm(heads(kd, D_KV_HEADS), qk_norm_g[1]), cos, sin)
    od = gqa_attention(qd, kd, heads(vd, D_KV_HEADS))
    od = od.transpose(0, 2, 1, 3).reshape(B, S, DQ_W)
    return jnp.concatenate([oc, od], axis=-1) @ w_out


def swiglu(h, w_gate, w_up, w_down):
    return (jax.nn.silu(h @ w_gate) * (h @ w_up)) @ w_down


def trunk(x, attn_norm_g, ev_w_in, ev_lambda, ev_subln_g, ev_rpb, ev_w_out,
          od_w_in, od_qk_norm_g, od_w_out, ffn_norm_g, ffn_w_gate, ffn_w_up, ffn_w_down,
          final_norm_g):
    h = x
    for i in range(DEPTH):
        hn = rms_norm(h, attn_norm_g[i])
        j = i // 2
        if i % 2 == 0:
            h = h + even_mixer(hn, ev_w_in[j], ev_lambda[j], ev_subln_g[j], ev_rpb[j],
                               ev_w_out[j], lambda_init_fn(i))
        else:
            h = h + odd_mixer(hn, od_w_in[j], od_qk_norm_g[j], od_w_out[j])
        h = h + swiglu(rms_norm(h, ffn_norm_g[i]), ffn_w_gate[i], ffn_w_up[i], ffn_w_down[i])
    return rms_norm(h, final_norm_g)


def setup_inputs(seed: int = 0) -> dict:
    key = jax.random.key(seed)
    ks = jax.random.split(key, 17)
    f32 = jnp.float32

    def nrm(k, shape, scale):
        return jax.random.normal(k, shape, f32) * scale

    D = D_MODEL
    return {
        "x_prompt": nrm(ks[0], (BATCH, SEQ, D), 1.0),
        "x_sample": nrm(ks[1], (DEC_BATCH, DEC_SEQ, D), 1.0),
        "attn_norm_g": 1.0 + nrm(ks[2], (DEPTH, D), 0.02),
        "ev_w_in": nrm(ks[3], (N_EVEN, D, EV_IN), D ** -0.5),
        "ev_lambda": nrm(ks[4], (N_EVEN, 4, HEAD_DIM), 0.1),
        "ev_subln_g": 1.0 + nrm(ks[5], (N_EVEN, 2 * HEAD_DIM), 0.02),
        "ev_rpb": nrm(ks[6], (N_EVEN, B_HEADS, 2 * NA_KH_MAX - 1, 2 * NA_KW - 1), 0.1),
        "ev_w_out": nrm(ks[7], (N_EVEN, EV_OUT, D), EV_OUT ** -0.5),
        "od_w_in": nrm(ks[8], (N_ODD, D, OD_IN), D ** -0.5),
        "od_qk_norm_g": 1.0 + nrm(ks[9], (N_ODD, 2, HEAD_DIM), 0.02),
        "od_w_out": nrm(ks[10], (N_ODD, OD_OUT, D), OD_OUT ** -0.5),
        "ffn_norm_g": 1.0 + nrm(ks[11], (DEPTH, D), 0.02),
        "ffn_w_gate": nrm(ks[12], (DEPTH, D, FFN_HIDDEN), D ** -0.5),
        "ffn_w_up": nrm(ks[13], (DEPTH, D, FFN_HIDDEN), D ** -0.5),
        "ffn_w_down": nrm(ks[14], (DEPTH, FFN_HIDDEN, D), FFN_HIDDEN ** -0.5),
        "final_norm_g": 1.0 + nrm(ks[15], (D,), 0.02),
    }


def reference(x_prompt, x_sample, attn_norm_g, ev_w_in, ev_lambda, ev_subln_g, ev_rpb, ev_w_out,
              od_w_in, od_qk_norm_g, od_w_out, ffn_norm_g, ffn_w_gate, ffn_w_up, ffn_w_down,
              final_norm_g):
    y_prompt = trunk(x_prompt, attn_norm_g, ev_w_in, ev_lambda, ev_subln_g, ev_rpb, ev_w_out,
                     od_w_in, od_qk_norm_g, od_w_out, ffn_norm_g, ffn_w_gate, ffn_w_up, ffn_w_down,
                     final_norm_g)
    y_sample = trunk(x_sample, attn_norm_g, ev_w_in, ev_lambda, ev_subln_g, ev_rpb, ev_w_out,
                     od_w_in, od_qk_norm_g, od_w_out, ffn_norm_g, ffn_w_gate, ffn_w_up, ffn_w_down,
                     final_norm_g)
    return (y_prompt, y_sample)
```

```python
import math
import numpy as np
import ml_dtypes
import concourse.bass as bass
import concourse.mybir as mybir
from concourse.bass_utils import run_bass_kernel_spmd

F32 = mybir.dt.float32
BF16 = mybir.dt.bfloat16
AF = mybir.ActivationFunctionType
ALU = mybir.AluOpType
AX = mybir.AxisListType

D = 2048
KC = 16
HD = 128
EPS = 1e-6
SCALE = HD ** -0.5
NEG = -30000.0
FFN = 5632
FC = 44
TA = 1024
SAME_ENGINE_SYNC = True


class Res:
    __slots__ = ("name", "lw", "ls", "rd", "dsem", "dcnt")

    def __init__(self, name):
        self.name = name
        self.lw = {}
        self.ls = {}
        self.rd = {}
        self.dsem = None
        self.dcnt = 0


class Sched:
    ENGS = ("pe", "act", "dve", "pool", "sp")

    def __init__(self, nc):
        self.nc = nc
        self.sem = {e: nc.alloc_semaphore("prog_" + e) for e in ("pe", "act", "dve", "pool")}
        self.nsig = {e: 0 for e in self.sem}
        self.waited = {e: {} for e in self.ENGS}
        self.prog = {e: [] for e in self.ENGS}
        self.engsem = {id(s): e for e, s in self.sem.items()}
        self.nres = 0
        self.nsem = 4

    def res(self, name=None, dma=False):
        self.nres += 1
        r = Res(name or f"r{self.nres}")
        if dma:
            r.dsem = self.nc.alloc_semaphore(f"d{self.nres}_" + r.name)
            self.nsem += 1
        return r

    def _collect(self, eng, reads, writes, swrites=()):
        deps = {}

        def need(t):
            s, v = t
            k = id(s)
            if k not in deps or deps[k][1] < v:
                deps[k] = (s, v)

        for r in reads:
            for t in r.lw.values():
                need(t)
            for t in r.ls.values():
                need(t)
        for w in writes:
            for t in w.lw.values():
                need(t)
            for t in w.ls.values():
                need(t)
            for t in w.rd.values():
                need(t)
        for w in swrites:
            for t in w.lw.values():
                need(t)
            for t in w.rd.values():
                need(t)
        waits = []
        wd = self.waited[eng]
        for k, (s, v) in deps.items():
            if wd.get(k, 0) >= v:
                continue
            src = self.engsem.get(k)
            if src is not None:
                if src == eng and (eng == "pe" or not SAME_ENGINE_SYNC):
                    continue
                if v > self.nsig[src]:
                    raise RuntimeError(f"unsignaled dep: {eng} waits {src}@{v} nsig={self.nsig[src]}")
            wd[k] = v
            waits.append((s, v))
        return waits

    def _mark(self, tk, reads, writes, swrites):
        s, v = tk
        k = id(s)
        for r in reads:
            r.rd[k] = tk
        for w in writes:
            w.lw = {k: tk}
            w.ls = {}
            w.rd = {}
        for w in swrites:
            w.ls[k] = tk

    def op(self, eng, fn, reads=(), writes=(), swrites=(), signal=True):
        waits = self._collect(eng, reads, writes, swrites)
        if signal:
            self.nsig[eng] += 1
            tk = (self.sem[eng], self.nsig[eng])
        else:
            tk = (self.sem[eng], self.nsig[eng] + 1)
        self.prog[eng].append((waits, fn, (self.sem[eng], 1) if signal else None))
        self._mark(tk, reads, writes, swrites)
        return tk

    def dma(self, q, fn, slot, reads=(), writes=(), swrites=()):
        waits = self._collect(q, reads, writes, swrites)
        slot.dcnt += 16
        tk = (slot.dsem, slot.dcnt)
        self.prog[q].append((waits, fn, (slot.dsem, 16)))
        self._mark(tk, reads, writes, swrites)
        return tk

    def barrier(self, resources, engines=None):
        for e in (engines or self.ENGS):
            waits = self._collect(e, (), resources)
            if waits:
                self.prog[e].append((waits, None, None))

    def emit(self):
        nc = self.nc
        prog = self.prog

        def run(engobj, lst):
            for waits, fn, sig in lst:
                for s, v in waits:
                    engobj.wait_ge(s, v)
                if fn is None:
                    continue
                ins = fn(engobj)
                if sig is not None:
                    ins.then_inc(sig[0], sig[1])

        with nc.Block() as block:
            @block.tensor
            def _(e):
                run(e, prog["pe"])

            @block.scalar
            def _(e):
                run(e, prog["act"])

            @block.vector
            def _(e):
                run(e, prog["dve"])

            @block.gpsimd
            def _(e):
                run(e, prog["pool"])

            @block.sync
            def _(e):
                run(e, prog["sp"])


def _dsize(dt):
    return 4 if dt == F32 else 2


class Arena:
    def __init__(self, nc, base, limit):
        self.nc, self.base, self.off, self.limit = nc, base, base, limit
        self.n = 0
        self.addr = {}

    def alloc(self, name, shape, dtype):
        size = int(np.prod(shape[1:])) * _dsize(dtype)
        size = (size + 63) // 64 * 64
        assert self.off + size <= self.limit, f"SBUF overflow at {name}: {self.off}+{size}>{self.limit}"
        self.n += 1
        t = self.nc.alloc_sbuf_tensor_at(f"{name}_{self.base}_{self.n}", list(shape), dtype, offset=self.off)
        self.addr[id(t)] = self.off
        self.off += size
        return t

    def reset(self):
        self.off = self.base


class Ring:
    def __init__(self, S, arena, name, n, shape, dtype, dma=False, reslist=None):
        self.slots = []
        for i in range(n):
            t = arena.alloc(f"{name}{i}", shape, dtype)
            r = S.res(f"{name}{i}", dma=dma)
            self.slots.append((t, r))
            if reslist is not None:
                reslist.append(r)
        self.i = 0

    def next(self):
        s = self.slots[self.i % len(self.slots)]
        self.i += 1
        return s


def _alibi(n):
    return [2.0 ** (-8.0 * (i + 1) / n) for i in range(n)]


def host_consts():
    c = {}
    c["c_ident"] = np.eye(128, dtype=np.float32)
    kl = np.arange(128)[:, None].astype(np.float32)
    ql = np.arange(512)[None, :].astype(np.float32)
    Dm = ql - kl
    dq = np.zeros((128, 5, 512), np.float32)
    dq[:, 0] = Dm
    for j, d0 in enumerate((0, -128, -256, -384)):
        dq[:, 1 + j] = np.abs(Dm + d0)
    c["c_dq"] = dq
    sl = _alibi(12)
    dils = (1, 4, 16)
    bd = np.zeros((128, 12, 2, 128), np.float32)
    k2 = np.arange(128)[:, None]
    q2 = np.arange(128)[None, :]
    for h in range(12):
        dil = dils[h // 4]
        rel0 = (k2 - q2 - 64).astype(np.float32)
        rel1 = (k2 - q2 + 64).astype(np.float32)
        bd[:, h, 0] = np.where(k2 >= q2, -sl[h] * dil * np.abs(rel0), NEG)
        bd[:, h, 1] = np.where(k2 <= q2, -sl[h] * dil * np.abs(rel1), NEG)
    c["c_bd"] = bd
    qc = np.arange(64)
    c0 = np.clip(qc - 8, 0, 48)
    kc = np.arange(64)
    valid = (kc[:, None] >= c0[None, :]) & (kc[:, None] < c0[None, :] + 16)
    cm = np.where(valid, 0.0, NEG).astype(np.float32)
    c["c_cm"] = np.concatenate([cm, cm], axis=0)
    t = np.arange(4096)
    row = (t // 64).astype(np.float32)
    col = (t % 64).astype(np.float32)
    f_row = (10000.0 ** (-np.arange(0, 64, 2, dtype=np.float32) / 64)).astype(np.float32)
    f_col = (10000.0 ** (-np.arange(0, 64, 2, dtype=np.float32) / 64)).astype(np.float32)
    ang = np.concatenate([row[:, None] * f_row[None, :], col[:, None] * f_col[None, :]], axis=-1).astype(np.float32)
    c["c_cos"] = np.cos(ang).astype(np.float32)
    c["c_sin"] = np.sin(ang).astype(np.float32)
    return c


def rpb_layout(rpb):
    rpb = np.asarray(rpb, np.float32)[0]
    kc = np.arange(64)[:, None]
    qc = np.arange(64)[None, :]
    idx = np.clip(15 + kc - qc, 0, 30)
    out = np.zeros((8, 128, 14, 64), np.float32)
    for krl in range(2):
        for r0 in range(14):
            out[:, krl * 64:(krl + 1) * 64, r0, :] = rpb[:, r0 + krl][:, idx]
    return out


class Builder:
    def __init__(self, seqs, layers=(0, 1), dbg=None):
        self.seqs = seqs
        self.layers = layers
        self.dbg = dbg or {}
        self.ntok = sum(seqs)
        self.tokbase = [sum(seqs[:i]) for i in range(len(seqs))]
        nc = self.nc = bass.Bass("TRN2", target_bir_lowering=False)
        self.S = Sched(nc)
        self._decl_io()
        self._alloc()

    def _decl_io(self):
        nc = self.nc
        inp = lambda n, sh: nc.dram_tensor(n, list(sh), F32, kind="ExternalInput").ap()
        self.x = [inp(f"x{i}", (s, D)) for i, s in enumerate(self.seqs)]
        self.y = [nc.dram_tensor(f"y{i}", [s, D], F32, kind="ExternalOutput").ap() for i, s in enumerate(self.seqs)]
        self.w = {}
        for n, sh in (("attn_norm_g", (2, D)), ("ev_w_in", (D, 6144)), ("ev_lambda", (4, 128)),
                      ("ev_subln_g", (1, 256)), ("rpbx", (8, 128, 14 * 64)), ("ev_w_out", (D, D)),
                      ("od_w_in", (D, 7168)), ("od_qk_norm_g", (2, 128)), ("od_w_out", (D, D)),
                      ("ffn_norm_g", (2, D)), ("ffn_w_gate", (2, D, FFN)), ("ffn_w_up", (2, D, FFN)),
                      ("ffn_w_down", (2, FFN, D)), ("final_norm_g", (1, D)),
                      ("c_ident", (128, 128)), ("c_dq", (128, 5 * 512)), ("c_bd", (128, 12 * 256)),
                      ("c_cm", (128, 64)), ("c_cos", (4096, 64)), ("c_sin", (4096, 64)), ("c_ab", (128, 128))):
            self.w[n] = inp(n, sh)
        nt = self.ntok
        kd = "ExternalOutput" if self.dbg.get("dump") else "Internal"
        self.HT = nc.dram_tensor("HT", [KC, 128, nt], F32, kind=kd).ap()
        self.QKT = [nc.dram_tensor(f"QKT{i}", [40, 128, s], BF16, kind=kd).ap() for i, s in enumerate(self.seqs)]
        self.VTM = [nc.dram_tensor(f"VTM{i}", [s, 2048], BF16, kind=kd).ap() for i, s in enumerate(self.seqs)]
        self.AO = [nc.dram_tensor(f"AO{i}", [s, 2048], BF16, kind=kd).ap() for i, s in enumerate(self.seqs)]
        self.CN = [nc.dram_tensor(f"CN{i}", [3, s, 4, 132], F32).ap() for i, s in enumerate(self.seqs)]
        S = self.S
        self.r_HT = [[S.res(f"HT{t}_{c}") for c in range(KC)] for t in range(nt // 512)]
        self.r_QKT = [S.res(f"QKT{i}") for i in range(len(self.seqs))]
        self.r_VTM = [S.res(f"VTM{i}") for i in range(len(self.seqs))]
        self.r_AO = [S.res(f"AO{i}") for i in range(len(self.seqs))]
        self.r_CN = [S.res(f"CN{i}") for i in range(len(self.seqs))]
        self.r_y = S.res("yout")

    def _alloc(self):
        nc, S = self.nc, self.S
        SB0 = 16640
        self.DYN0 = SB0 + 102 * 1024
        self.DYN1 = 229376
        A = self.A = Arena(nc, SB0, self.DYN0)
        self.ident_f = A.alloc("identf", [128, 128], F32)
        self.ident_b = A.alloc("identb", [128, 128], BF16)
        self.ones_b = A.alloc("onesb", [128, 128], BF16)
        self.gT = A.alloc("gT", [128, 5, KC], F32)
        self.epsT = A.alloc("epsT", [128, 1], F32)
        self.r_const = S.res("const", dma=True)
        self.r_const2 = S.res("const2", dma=True)
        self.hring = Ring(S, A, "hT", 6, [128, 512], F32, dma=True)
        self.actT = A.alloc("actT", [128, KC, TA], BF16)
        self.r_actT = [S.res("actT0"), S.res("actT1")]
        self.sqring = Ring(S, A, "sq", 2, [128, 512], BF16)
        self.lnt = A.alloc("lnt", [128, 512], F32)
        self.r_lnt = S.res("lnt")
        self.rstd = A.alloc("rstd", [128, 512], F32)
        self.r_rstd = S.res("rstd")
        self.wring = Ring(S, A, "w", 2, [128, KC, 512], BF16, dma=True)
        self.wd_views = []
        for (t, r) in self.wring.slots:
            off = A.addr[id(t)]
            self.wd_views.append(nc.alloc_sbuf_tensor_at(f"wdv{len(self.wd_views)}", [128, FC, 128], BF16, offset=off))
        self.stg_b = Ring(S, A, "stgb", 2, [128, 1024], BF16, dma=True)
        self.stg_f = Ring(S, A, "stgf", 3, [128, 512], F32, dma=True)
        self.aoin = Ring(S, A, "aoin", 2, [128, 2048], BF16, dma=True)
        self.static_end = A.off
        self.DA = Arena(nc, self.DYN0, self.DYN1)
        self.layouts = {}
        self.cur_layout = None
        self.ps = [nc.alloc_psum_tensor(f"ps{i}", [128, 512], F32) for i in range(8)]
        self.r_ps = [S.res(f"ps{i}") for i in range(8)]
        self.psbs = [self.ps[6], self.ps[7]]
        self.r_psb = [self.r_ps[6], self.r_ps[7]]
        self.psi = 0
        self.evi = 0
        self.normed = None
        self.rope_pending = []

    def use_layout(self, name, maker):
        if self.cur_layout == name:
            return self.layouts[name]
        self.layout_epoch = getattr(self, "layout_epoch", 0) + 1
        if self.cur_layout is not None:
            self.S.barrier(self.layouts[self.cur_layout]["_res"])
        if name not in self.layouts:
            self.DA.reset()
            L = {"_res": []}
            maker(L, self.DA)
            self.layouts[name] = L
        self.cur_layout = name
        return self.layouts[name]

    def gemm_bank(self):
        i = self.psi % 4
        self.psi += 1
        return self.ps[i], self.r_ps[i]

    def ev_eng(self):
        self.evi += 1
        return "act" if self.evi % 2 else "dve"

    def copy(self, eng, out, in_, reads, writes=(), swrites=()):
        if eng == "act":
            return self.S.op("act", lambda a: a.copy(out=out, in_=in_), reads=reads, writes=writes, swrites=swrites)
        return self.S.op(eng, lambda v: v.tensor_copy(out=out, in_=in_), reads=reads, writes=writes, swrites=swrites)

    def load_consts(self):
        S, w = self.S, self.w
        rc = self.r_const
        S.dma("sp", lambda q: q.dma_start(out=self.ident_f[:], in_=w["c_ident"]), rc, swrites=[rc])
        S.dma("pool", lambda q: q.dma_start(out=self.ident_b[:], in_=w["c_ident"]), self.r_const2, writes=[self.r_const2])
        S.op("dve", lambda v: v.memset(self.ones_b[:], 1.0), swrites=[rc])
        S.op("dve", lambda v: v.memset(self.epsT[:], EPS), swrites=[rc])
        for i, (n, row) in enumerate((("attn_norm_g", 0), ("attn_norm_g", 1), ("ffn_norm_g", 0),
                                      ("ffn_norm_g", 1), ("final_norm_g", 0))):
            src = w[n][row:row + 1, :].rearrange("o (c p) -> p (o c)", p=128)
            S.dma("sp", lambda q, src=src, i=i: q.dma_start(out=self.gT[:, i, :], in_=src,
                                                            allow_slow_non_contiguous=True), rc, swrites=[rc])

    def phase_in(self, si):
        S = self.S
        Sq = self.seqs[si]
        for b in range(Sq // 128):
            g0 = self.tokbase[si] + b * 128
            tile = g0 // 512
            xin = self.use_layout("io", self._mk_io)["xin"]
            xt, rx = xin.next()
            S.dma("sp", lambda q, xt=xt, b=b: q.dma_start(out=xt[:], in_=self.x[si][b * 128:(b + 1) * 128, :]),
                  rx, writes=[rx])
            for cg in range(4):
                bank, rb = self.ps[4 + cg % 2], self.r_ps[4 + cg % 2]
                for j in range(4):
                    c = cg * 4 + j
                    S.op("pe", lambda t, bank=bank, j=j, c=c, xt=xt: t.transpose(
                        bank[:, j * 128:(j + 1) * 128], xt[:, c * 128:(c + 1) * 128], self.ident_f[:]),
                        reads=[rx, self.r_const], writes=[rb], signal=(j == 3))
                st, rs = self.stg_f.next()
                self.copy(self.ev_eng(), st[:], bank[:], reads=[rb], writes=[rs])
                dst = self.HT[cg * 4:(cg + 1) * 4, :, g0:g0 + 128].rearrange("c p t -> p c t")
                S.dma("sp", lambda q, st=st, dst=dst: q.dma_start(
                    out=dst, in_=st[:].rearrange("p (c t) -> p c t", c=4)), rs, reads=[rs],
                    swrites=[self.r_HT[tile][cg * 4 + j] for j in range(4)])

    def _mk_io(self, L, A):
        L["xin"] = Ring(self.S, A, "xin", 2, [128, 2048], F32, dma=True, reslist=L["_res"])
        L["ynT"] = A.alloc("ynT", [128, KC, 512], F32)
        L["r_ynT"] = self.S.res("ynT")
        L["_res"].append(L["r_ynT"])

    def norm_half(self, h512, gi, dst_fn, r_dst):
        S = self.S
        g0 = h512 * 512
        bank, rb = self.ps[4], self.r_ps[4]
        for c in range(KC):
            hb, rh = self.hring.next()
            S.dma("sp", lambda q, hb=hb, c=c: q.dma_start(out=hb[:], in_=self.HT[c, :, g0:g0 + 512]),
                  rh, reads=[self.r_HT[h512][c]], writes=[rh])
            sq, rsq = self.sqring.next()
            S.op("act", lambda a, sq=sq, hb=hb: a.activation(out=sq[:], in_=hb[:], func=AF.Square),
                 reads=[rh], writes=[rsq])
            S.op("pe", lambda t, sq=sq, c=c: t.matmul(bank[:], lhsT=self.ones_b[:], rhs=sq[:],
                                                      start=(c == 0), stop=(c == KC - 1)),
                 reads=[rsq, self.r_const], writes=[rb], signal=True)
        S.op("act", lambda a: a.activation(out=self.lnt[:], in_=bank[:], func=AF.Ln,
                                           scale=1.0 / D, bias=self.eps_ap()),
             reads=[rb, self.r_const], writes=[self.r_lnt])
        S.op("act", lambda a: a.activation(out=self.rstd[:], in_=self.lnt[:], func=AF.Exp, scale=-0.5),
             reads=[self.r_lnt], writes=[self.r_rstd])
        for c in range(KC):
            hb, rh = self.hring.next()
            S.dma("sp", lambda q, hb=hb, c=c: q.dma_start(out=hb[:], in_=self.HT[c, :, g0:g0 + 512]),
                  rh, reads=[self.r_HT[h512][c]], writes=[rh])
            dst = dst_fn(c)
            S.op("dve", lambda v, hb=hb, c=c, dst=dst: v.scalar_tensor_tensor(
                out=dst, in0=hb[:], scalar=self.gT[:, gi, c:c + 1],
                in1=self.rstd[:], op0=ALU.mult, op1=ALU.mult),
                reads=[rh, self.r_rstd, self.r_const],
                writes=([r_dst] if c == 0 else ()), swrites=(() if c == 0 else [r_dst]))

    def norm_tile(self, tile, gi):
        for tt in range(TA // 512):
            self.norm_half(tile * 2 + tt, gi,
                           lambda c, tt=tt: self.actT[:, c, tt * 512:(tt + 1) * 512], self.r_actT[tt])

    def eps_ap(self):
        return self.epsT[:, 0:1]

    def load_w(self, wap, col0, ncols, slot=None, off=0):
        S = self.S
        if slot is None:
            slot = self.wring.next()
        wt, rw = slot
        src = wap[:, col0:col0 + ncols].rearrange("(c p) n -> p c n", p=128)
        S.dma("pool", lambda q, wt=wt, src=src: q.dma_start(out=wt[:, :, off:off + ncols], in_=src), rw,
              writes=([rw] if off == 0 else ()), swrites=(() if off == 0 else [rw]))
        return slot

    def mm_fm(self, wt, rw, wcol, tt, nk=KC, act=None, ract=None):
        bank, rb = self.gemm_bank()
        S = self.S
        act = act if act is not None else self.actT
        ract = ract if ract is not None else self.r_actT[tt]
        for c in range(nk):
            S.op("pe", lambda t, c=c, bank=bank, wt=wt, wcol=wcol, tt=tt, act=act: t.matmul(
                bank[:], lhsT=wt[:, c, wcol:wcol + 128], rhs=act[:, c, tt * 512:(tt + 1) * 512],
                start=(c == 0), stop=(c == nk - 1)),
                reads=[rw, ract], writes=[rb], signal=(c == nk - 1))
        return bank, rb

    def mm_tm(self, wt, rw, tb):
        bank, rb = self.gemm_bank()
        S = self.S
        tt = tb // 4
        for c in range(KC):
            S.op("pe", lambda t, c=c, bank=bank, wt=wt, tb=tb: t.matmul(
                bank[:], lhsT=self.actT[:, c, tb * 128:(tb + 1) * 128], rhs=wt[:, c, :],
                start=(c == 0), stop=(c == KC - 1)),
                reads=[rw, self.r_actT[tt]], writes=[rb], signal=(c == KC - 1))
        return bank, rb

    def phase_a(self, layer, si, tile):
        S = self.S
        even = (layer % 2 == 0)
        if self.normed != ("a", layer, tile):
            self.norm_tile(tile, layer)
        self.normed = None
        t0 = tile * TA - self.tokbase[si]
        wap = self.w["ev_w_in"] if even else self.w["od_w_in"]
        ncb = 12 if even else 14
        wslots = {0: self.load_w(wap, 0, 512)}
        for jb in range(ncb):
            col0 = jb * 512
            if jb + 1 < ncb:
                wslots[jb + 1] = self.load_w(wap, (jb + 1) * 512, 512)
            wt, rw = wslots.pop(jb)
            kind, blk0, vcol0 = self._col_kind(even, col0)
            if kind == "fm":
                for jj in range(4):
                    st, rs = self.stg_b.next()
                    for tt in range(TA // 512):
                        bank, rb = self.mm_fm(wt, rw, jj * 128, tt)
                        self.copy(self.ev_eng(), st[:, tt * 512:(tt + 1) * 512], bank[:], reads=[rb],
                                  writes=([rs] if tt == 0 else ()), swrites=(() if tt == 0 else [rs]))
                    dst = self.QKT[si][blk0 + jj, :, t0:t0 + TA]
                    S.dma("sp", lambda q, st=st, dst=dst: q.dma_start(out=dst, in_=st[:]), rs, reads=[rs],
                          swrites=[self.r_QKT[si]])
            elif kind == "tm":
                for tb in range(TA // 128):
                    bank, rb = self.mm_tm(wt, rw, tb)
                    st, rs = self.stg_b.next()
                    self.copy(self.ev_eng(), st[:, 0:512], bank[:], reads=[rb], writes=[rs])
                    dst = self.VTM[si][t0 + tb * 128:t0 + (tb + 1) * 128, vcol0:vcol0 + 512]
                    S.dma("sp", lambda q, st=st, dst=dst: q.dma_start(out=dst, in_=st[:, 0:512]), rs, reads=[rs],
                          swrites=[self.r_VTM[si]])
            else:
                self._rope_block(si, t0, wt, rw, blk0, 0 if col0 < 6144 else 1)
        self.rope_flush()

    def _col_kind(self, even, col0):
        if even:
            if col0 < 2048:
                return "fm", col0 // 128, None
            if col0 < 3072:
                return "tm", None, col0 - 2048
            if col0 < 5120:
                return "fm", 16 + (col0 - 3072) // 128, None
            return "tm", None, 1024 + (col0 - 5120)
        else:
            if col0 < 3072:
                return "fm", col0 // 128, None
            if col0 < 4608:
                return "tm", None, col0 - 3072
            if col0 < 6656:
                return "rope", 24 + (col0 - 4608) // 128, None
            return "tm", None, 1536 + (col0 - 6656)

    def _mk_rope(self, L, A):
        S, RL = self.S, L["_res"]
        L["stq"] = Ring(S, A, "stq", 2, [128, 4, 128], BF16, dma=True, reslist=RL)
        L["cs"] = Ring(S, A, "cs", 3, [128, 2, 64], F32, dma=True, reslist=RL)
        L["yb"] = Ring(S, A, "yb", 4, [128, 512], BF16, reslist=RL)
        for n, sh in (("sqf", [128, 512]), ("xn", [128, 512]), ("ta", [128, 256]), ("tb", [128, 256]),
                      ("tc", [128, 256]), ("td", [128, 256]),
                      ("ss", [128, 4]), ("ln", [128, 4]), ("rs", [128, 4])):
            L[n] = Ring(S, A, "rp_" + n, 2, sh, F32, reslist=RL)
        L["g"] = A.alloc("ropeg", [128, 2, 128], F32)
        L["r_g"] = S.res("ropeg", dma=True)
        RL.append(L["r_g"])
        L["g_loaded"] = False

    def _rope_block(self, si, t0, wt, rw, blk0, gidx):
        S = self.S
        R = self.use_layout("rope", self._mk_rope)
        if R["g_loaded"] != self.layout_epoch:
            R["g_loaded"] = self.layout_epoch
            src = self.w["od_qk_norm_g"].rearrange("(o a) d -> o a d", o=1).broadcast_to([128, 2, 128])
            S.dma("sp", lambda q: q.dma_start(out=R["g"][:], in_=src), R["r_g"], writes=[R["r_g"]])
        for tb in range(TA // 128):
            tok = t0 + tb * 128
            bank, rb = self.mm_tm(wt, rw, tb)
            cs, rcs = R["cs"].next()
            S.dma("sp", lambda q, cs=cs, tok=tok: q.dma_start(out=cs[:, 0, :], in_=self.w["c_cos"][tok:tok + 128, :]),
                  rcs, writes=[rcs])
            S.dma("sp", lambda q, cs=cs, tok=tok: q.dma_start(out=cs[:, 1, :], in_=self.w["c_sin"][tok:tok + 128, :]),
                  rcs, swrites=[rcs])
            sqf, r1 = R["sqf"].next()
            S.op("act", lambda a, bank=bank, sqf=sqf: a.activation(out=sqf[:], in_=bank[:], func=AF.Square),
                 reads=[rb], writes=[r1])
            ss, r2 = R["ss"].next()
            S.op("dve", lambda v, ss=ss, sqf=sqf: v.tensor_reduce(out=ss[:], in_=sqf[:].rearrange("p (h d) -> p h d", h=4),
                                                                  axis=AX.X, op=ALU.add), reads=[r1], writes=[r2])
            ln, rln = R["ln"].next()
            S.op("act", lambda a, ln=ln, ss=ss: a.activation(out=ln[:], in_=ss[:], func=AF.Ln, scale=1.0 / HD,
                                                             bias=self.eps_ap()), reads=[r2, self.r_const], writes=[rln])
            rs, rrs = R["rs"].next()
            S.op("act", lambda a, rs=rs, ln=ln: a.activation(out=rs[:], in_=ln[:], func=AF.Exp, scale=-0.5),
                 reads=[rln], writes=[rrs])
            xn, rxn = R["xn"].next()
            for hh in range(4):
                S.op("dve", lambda v, bank=bank, xn=xn, rs=rs, hh=hh: v.scalar_tensor_tensor(
                    out=xn[:, hh * 128:(hh + 1) * 128], in0=bank[:, hh * 128:(hh + 1) * 128], scalar=rs[:, hh:hh + 1],
                    in1=R["g"][:, gidx, :], op0=ALU.mult, op1=ALU.mult),
                    reads=[rb, rrs, R["r_g"]], writes=([rxn] if hh == 0 else ()), swrites=(() if hh == 0 else [rxn]))
            pr = lambda t: t[:].rearrange("p (h i two) -> p h i two", h=4, two=2)
            x0, x1 = pr(xn)[:, :, :, 0], pr(xn)[:, :, :, 1]
            cosb = cs[:, 0, :].unsqueeze(1).broadcast_to([128, 4, 64])
            sinb = cs[:, 1, :].unsqueeze(1).broadcast_to([128, 4, 64])
            (ta, r_ta), (tb_, r_tb), (tc, r_tc), (td, r_td) = R["ta"].next(), R["tb"].next(), R["tc"].next(), R["td"].next()
            yb, ryb = R["yb"].next()
            y0, y1 = pr(yb)[:, :, :, 0], pr(yb)[:, :, :, 1]
            v3 = lambda t: t[:].rearrange("p (h i) -> p h i", h=4)
            S.op("dve", lambda v, x0=x0, cosb=cosb, ta=ta: v.tensor_tensor(out=v3(ta), in0=x0, in1=cosb, op=ALU.mult),
                 reads=[rxn, rcs], writes=[r_ta])
            S.op("dve", lambda v, x1=x1, sinb=sinb, tb_=tb_: v.tensor_tensor(out=v3(tb_), in0=x1, in1=sinb, op=ALU.mult),
                 reads=[rxn, rcs], writes=[r_tb])
            S.op("dve", lambda v, y0=y0, ta=ta, tb_=tb_: v.tensor_tensor(out=y0, in0=v3(ta), in1=v3(tb_), op=ALU.subtract),
                 reads=[r_ta, r_tb], writes=[ryb])
            S.op("pool", lambda v, x0=x0, sinb=sinb, tc=tc: v.tensor_tensor(out=v3(tc), in0=x0, in1=sinb, op=ALU.mult),
                 reads=[rxn, rcs], writes=[r_tc])
            S.op("pool", lambda v, x1=x1, cosb=cosb, td=td: v.tensor_tensor(out=v3(td), in0=x1, in1=cosb, op=ALU.mult),
                 reads=[rxn, rcs], writes=[r_td])
            S.op("pool", lambda v, y1=y1, tc=tc, td=td: v.tensor_tensor(out=y1, in0=v3(tc), in1=v3(td), op=ALU.add),
                 reads=[r_tc, r_td], swrites=[ryb])
            def back(tb=tb, yb=yb, ryb=ryb, tok=tok):
                pb, rpb_ = self.psbs[tb % 2], self.r_psb[tb % 2]
                for hh in range(4):
                    S.op("pe", lambda t, hh=hh: t.matmul(
                        pb[:, hh * 128:(hh + 1) * 128], lhsT=yb[:, hh * 128:(hh + 1) * 128], rhs=self.ident_b[:],
                        start=True, stop=True),
                        reads=[ryb, self.r_const2], writes=[rpb_], signal=(hh == 3))
                stq, rsq_ = R["stq"].next()
                self.copy(self.ev_eng(), stq[:], pb[:, 0:512].rearrange("p (h t) -> p h t", h=4),
                          reads=[rpb_], writes=[rsq_])
                dst = self.QKT[si][blk0:blk0 + 4, :, tok:tok + 128].rearrange("h p t -> p h t")
                S.dma("sp", lambda q: q.dma_start(out=dst, in_=stq[:]), rsq_, reads=[rsq_],
                      swrites=[self.r_QKT[si]])
            self.rope_pending.append(back)
            while len(self.rope_pending) > 2:
                self.rope_pending.pop(0)()

    def rope_flush(self):
        while self.rope_pending:
            self.rope_pending.pop(0)()

    def resid_store(self, bank, rb, ob, h512):
        S = self.S
        g0 = h512 * 512
        hb, rh = self.hring.next()
        S.dma("sp", lambda q, hb=hb: q.dma_start(out=hb[:], in_=self.HT[ob, :, g0:g0 + 512]),
              rh, reads=[self.r_HT[h512][ob]], writes=[rh])
        st, rs = self.stg_f.next()
        S.op("dve", lambda v, st=st, hb=hb, bank=bank: v.tensor_tensor(out=st[:], in0=bank[:], in1=hb[:], op=ALU.add),
             reads=[rb, rh], writes=[rs])
        S.dma("sp", lambda q, st=st: q.dma_start(out=self.HT[ob, :, g0:g0 + 512], in_=st[:]),
              rs, reads=[rs], swrites=[self.r_HT[h512][ob]])

    def phase_c(self, layer, si, tile):
        S = self.S
        t0 = tile * TA - self.tokbase[si]
        for tb in range(TA // 128):
            ao, rao = self.aoin.next()
            S.dma("sp", lambda q, ao=ao, tb=tb: q.dma_start(
                out=ao[:], in_=self.AO[si][t0 + tb * 128:t0 + (tb + 1) * 128, :]),
                rao, reads=[self.r_AO[si]], writes=[rao])
            for cg in range(4):
                half = cg % 2
                po = 0
                pb = self.psbs[half]
                for j in range(4):
                    c = cg * 4 + j
                    S.op("pe", lambda t, ao=ao, c=c, j=j, po=po, pb=pb: t.matmul(
                        pb[:, po + j * 128:po + (j + 1) * 128], lhsT=ao[:, c * 128:(c + 1) * 128], rhs=self.ident_b[:],
                        start=True, stop=True),
                        reads=[rao, self.r_const2], writes=[self.r_psb[half]], signal=(j == 3))
                first = (tb % 4 == 0 and cg == 0)
                ra = self.r_actT[tb // 4]
                self.copy(self.ev_eng(), self.actT[:, cg * 4:(cg + 1) * 4, tb * 128:(tb + 1) * 128],
                          pb[:, po:po + 512].rearrange("p (c t) -> p c t", c=4),
                          reads=[self.r_psb[half]], writes=([ra] if first else ()), swrites=(() if first else [ra]))
        wap = self.w["ev_w_out"] if layer % 2 == 0 else self.w["od_w_out"]
        for jb in range(4):
            wt, rw = self.load_w(wap, jb * 512, 512)
            for jj in range(4):
                ob = jb * 4 + jj
                for tt in range(TA // 512):
                    bank, rb = self.mm_fm(wt, rw, jj * 128, tt)
                    self.resid_store(bank, rb, ob, tile * 2 + tt)

    def _mk_ffn(self, L, A):
        S = self.S
        L["aT"] = A.alloc("aT", [128, FC, TA], BF16)
        L["r_aT"] = [S.res("aT0"), S.res("aT1")]
        L["_res"] += L["r_aT"]
        L["sg"] = Ring(S, A, "sg", 2, [128, 512], F32, reslist=L["_res"])

    def phase_d(self, layer, tile, nxt=None):
        S = self.S
        if self.normed != ("d", layer, tile):
            self.norm_tile(tile, 2 + layer)
        self.normed = None
        L = self.use_layout("ffn", self._mk_ffn)
        aT, r_aT = L["aT"], L["r_aT"]
        wg, wu, wd = self.w["ffn_w_gate"][layer], self.w["ffn_w_up"][layer], self.w["ffn_w_down"][layer]
        for fp in range(FC // 2):
            slot = self.wring.next()
            self.load_w(wg, fp * 256, 256, slot, off=0)
            self.load_w(wu, fp * 256, 256, slot, off=256)
            wt, rw = slot
            for j in range(2):
                fb = fp * 2 + j
                for tt in range(TA // 512):
                    bg, rbg = self.mm_fm(wt, rw, j * 128, tt)
                    bu, rbu = self.mm_fm(wt, rw, 256 + j * 128, tt)
                    sg, rsg = L["sg"].next()
                    S.op("act", lambda a, sg=sg, bg=bg: a.activation(out=sg[:], in_=bg[:], func=AF.Silu),
                         reads=[rbg], writes=[rsg])
                    first = (fb == 0)
                    S.op("dve", lambda v, sg=sg, bu=bu, fb=fb, tt=tt: v.tensor_tensor(
                        out=aT[:, fb, tt * 512:(tt + 1) * 512], in0=sg[:], in1=bu[:], op=ALU.mult),
                        reads=[rsg, rbu], writes=([r_aT[tt]] if first else ()), swrites=(() if first else [r_aT[tt]]))
        for ob in range(KC):
            if nxt is not None and ob in (2, 9):
                tt_ = 0 if ob == 2 else 1
                self.norm_half(nxt[2] * 2 + tt_, nxt[1] if nxt[0] == "a" else 2 + nxt[1],
                               lambda c, tt_=tt_: self.actT[:, c, tt_ * 512:(tt_ + 1) * 512], self.r_actT[tt_])
                self.normed = nxt
            slot = self.wring.next()
            idx = (self.wring.i - 1) % len(self.wring.slots)
            wv = self.wd_views[idx]
            _, rw = slot
            src = wd[:, ob * 128:(ob + 1) * 128].rearrange("(c p) n -> p c n", p=128)
            S.dma("pool", lambda q, wv=wv, src=src: q.dma_start(out=wv[:], in_=src), rw, writes=[rw])
            for tt in range(TA // 512):
                bank, rb = self.gemm_bank()
                for fc in range(FC):
                    S.op("pe", lambda t, fc=fc, bank=bank, wv=wv, tt=tt: t.matmul(
                        bank[:], lhsT=wv[:, fc, :], rhs=aT[:, fc, tt * 512:(tt + 1) * 512],
                        start=(fc == 0), stop=(fc == FC - 1)),
                        reads=[rw, r_aT[tt]], writes=[rb], signal=(fc == FC - 1))
                self.resid_store(bank, rb, ob, tile * 2 + tt)

    def phase_out(self, si):
        S = self.S
        L = self.use_layout("io", self._mk_io)
        ynT, r_ynT = L["ynT"], L["r_ynT"]
        Sq = self.seqs[si]
        for ht in range(Sq // 512):
            h512 = self.tokbase[si] // 512 + ht
            self.norm_half(h512, 4, lambda c: ynT[:, c, :], r_ynT)
            for tb in range(4):
                xt, rx = L["xin"].next()
                for cg in range(4):
                    bank, rb = self.ps[4 + cg % 2], self.r_ps[4 + cg % 2]
                    for j in range(4):
                        c = cg * 4 + j
                        S.op("pe", lambda t, bank=bank, j=j, c=c, tb=tb: t.transpose(
                            bank[:, j * 128:(j + 1) * 128], ynT[:, c, tb * 128:(tb + 1) * 128], self.ident_f[:]),
                            reads=[r_ynT, self.r_const], writes=[rb], signal=(j == 3))
                    first = (cg == 0)
                    self.copy(self.ev_eng(), xt[:, cg * 512:(cg + 1) * 512], bank[:], reads=[rb],
                              writes=([rx] if first else ()), swrites=(() if first else [rx]))
                r0 = ht * 512 + tb * 128
                S.dma("sp", lambda q, xt=xt, r0=r0: q.dma_start(out=self.y[si][r0:r0 + 128, :], in_=xt[:]),
                      rx, reads=[rx], swrites=[self.r_y])

    def _mk_dense(self, L, A):
        S, RL = self.S, L["_res"]
        L["k"] = Ring(S, A, "dk", 4, [128, 4096], BF16, dma=True, reslist=RL)
        L["q"] = Ring(S, A, "dq_", 4, [128, 512], BF16, dma=True, reslist=RL)
        L["v"] = Ring(S, A, "dv", 2, [128, 32, 257], BF16, dma=True, reslist=RL)
        L["pt"] = Ring(S, A, "dpt", 4, [128, 512], BF16, reslist=RL)
        L["x"] = Ring(S, A, "dx", 2, [128, 512], F32, reslist=RL)
        L["ostg"] = Ring(S, A, "dost", 2, [128, 4, 256], BF16, dma=True, reslist=RL)
        L["dq"] = A.alloc("dqc", [128, 5, 512], F32)
        L["ab"] = A.alloc("abias", [128, 4, 32], F32)
        L["lamT"] = A.alloc("lamT", [128, 4, 128], F32)
        L["gsub"] = A.alloc("gsub", [128, 256], F32)
        L["r_c"] = S.res("densec", dma=True)
        RL.append(L["r_c"])
        for n, sh in (("o1", [128, 4, 256]), ("of", [128, 256]), ("junk", [128, 256]), ("rden", [128, 4]),
                      ("nlr", [128, 4]), ("ssq", [128, 1]), ("lnq", [128, 1]), ("rsq", [128, 1]),
                      ("lp", [128, 2, 128]), ("ls", [128, 2]), ("le", [128, 2]), ("nlam", [128, 1])):
            L[n] = A.alloc("d_" + n, sh, F32)
            L["r_" + n] = S.res("d_" + n)
            RL.append(L["r_" + n])

    def dense_attn(self, si, mode, layer):
        S = self.S
        Sq = self.seqs[si]
        nkb, nqt = Sq // 128, Sq // 512
        L = self.use_layout("dense_" + mode, self._mk_dense)
        dv = 257 if mode == "diff" else 129
        rc = L["r_c"]
        for (vt, rv) in L["v"].slots:
            S.op("dve", lambda v, vt=vt: v.memset(vt[:, :, dv - 1:dv], 1.0), writes=[rv])
        if mode == "diff":
            S.dma("sp", lambda q: q.dma_start(out=L["dq"][:].rearrange("p a b -> p (a b)"), in_=self.w["c_dq"]),
                  rc, writes=[rc])
            S.dma("sp", lambda q: q.dma_start(out=L["ab"][:].rearrange("p a b -> p (a b)"), in_=self.w["c_ab"]),
                  rc, swrites=[rc])
            S.dma("sp", lambda q: q.dma_start(
                out=L["lamT"][:], in_=self.w["ev_lambda"].rearrange("(o a) d -> o a d", o=1).broadcast_to([128, 4, 128])),
                rc, swrites=[rc])
            S.dma("sp", lambda q: q.dma_start(out=L["gsub"][:], in_=self.w["ev_subln_g"].broadcast_to([128, 256])),
                  rc, swrites=[rc])
            lam_init = 0.8 - 0.6 * math.exp(-0.3 * layer)
            lamT = L["lamT"]
            S.op("dve", lambda v: v.tensor_tensor(out=L["lp"][:], in0=lamT[:, 0:4:2, :], in1=lamT[:, 1:4:2, :],
                                                  op=ALU.mult), reads=[rc], writes=[L["r_lp"]])
            S.op("dve", lambda v: v.tensor_reduce(out=L["ls"][:], in_=L["lp"][:], axis=AX.X, op=ALU.add),
                 reads=[L["r_lp"]], writes=[L["r_ls"]])
            S.op("act", lambda a: a.activation(out=L["le"][:], in_=L["ls"][:], func=AF.Exp),
                 reads=[L["r_ls"]], writes=[L["r_le"]])
            S.op("dve", lambda v: v.scalar_tensor_tensor(out=L["nlam"][:], in0=L["le"][:, 1:2], scalar=-lam_init,
                                                         in1=L["le"][:, 0:1], op0=ALU.add, op1=ALU.subtract),
                 reads=[L["r_le"]], writes=[L["r_nlam"]])
            S.op("dve", lambda v: v.tensor_scalar(out=L["gsub"][:], in0=L["gsub"][:], scalar1=1.0 - lam_init,
                                                  scalar2=None, op0=ALU.mult), reads=[rc], writes=[rc])
            slopes = _alibi(4)
        accs = [2, 3, 4, 5, 6, 7]
        acc_i = [0]

        def next_acc():
            i = accs[acc_i[0] % len(accs)]
            acc_i[0] += 1
            return self.ps[i], self.r_ps[i]

        sc_i = [0]
        steps = []
        prel = {}
        steps_done = []
        AOs = self.AO[si]

        def add_tile(pre, kt, rk, qsrc, vt, rv, qt, h, fin):
            st = {"banks": None, "q": None}
            if pre is not None:
                prel.setdefault(max(0, len(steps) - 24), []).append(pre)

            def mk(kb):
                hold = {}

                def front():
                    for p_ in prel.pop(len(steps_done), []):
                        p_()
                    steps_done.append(1)
                    if kb == 0:
                        qtile, rq = L["q"].next()
                        S.dma("sp", lambda q, qtile=qtile: q.dma_start(out=qtile[:], in_=qsrc),
                              rq, reads=[self.r_QKT[si]], writes=[rq])
                        st["q"] = (qtile, rq)
                    qtile, rq = st["q"]
                    sc, rsc = self.ps[sc_i[0] % 2], self.r_ps[sc_i[0] % 2]
                    sc_i[0] += 1
                    S.op("pe", lambda t, sc=sc, qtile=qtile: t.matmul(
                        sc[:], lhsT=kt[:, kb * 128:(kb + 1) * 128], rhs=qtile[:], start=True, stop=True),
                        reads=[rk, rq], writes=[rsc])
                    pt, rpt = L["pt"].next()
                    if mode == "diff":
                        d0 = 512 * qt - 128 * kb
                        sl = slopes[h]
                        if d0 >= 128:
                            dsel, coef, bcol = 0, -sl / SCALE, d0 // 128
                        elif d0 <= -512:
                            dsel, coef, bcol = 0, sl / SCALE, (-d0) // 128
                        else:
                            dsel, coef, bcol = 1 + (-d0) // 128, -sl / SCALE, 0
                        xt, rx = L["x"].next()
                        S.op("dve", lambda v, xt=xt, sc=sc: v.scalar_tensor_tensor(
                            out=xt[:], in0=L["dq"][:, dsel, :], scalar=coef, in1=sc[:], op0=ALU.mult, op1=ALU.add),
                            reads=[rsc, rc], writes=[rx])
                        S.op("act", lambda a, pt=pt, xt=xt: a.activation(
                            out=pt[:], in_=xt[:], func=AF.Exp, scale=SCALE, bias=L["ab"][:, h, bcol:bcol + 1]),
                            reads=[rx, rc], writes=[rpt])
                    else:
                        S.op("act", lambda a, pt=pt, sc=sc: a.activation(out=pt[:], in_=sc[:], func=AF.Exp, scale=SCALE),
                             reads=[rsc], writes=[rpt])
                    hold["pt"] = (pt, rpt)

                def back():
                    if kb == 0:
                        st["banks"] = [next_acc() for _ in range(4)]
                    pt, rpt = hold["pt"]
                    for qs in range(4):
                        bk, rbk = st["banks"][qs]
                        S.op("pe", lambda t, bk=bk, pt=pt, qs=qs: t.matmul(
                            bk[:, 0:dv], lhsT=pt[:, qs * 128:(qs + 1) * 128], rhs=vt[:, kb, 0:dv],
                            start=(kb == 0), stop=(kb == nkb - 1)),
                            reads=[rpt, rv], writes=[rbk], signal=(qs == 3))
                    if kb == nkb - 1:
                        fin(st["banks"])
                return front, back

            for kb in range(nkb):
                steps.append(mk(kb))

        def run_pipeline(lag):
            n = len(steps)
            for i in range(n + lag):
                if i < n:
                    steps[i][0]()
                if i - lag >= 0:
                    steps[i - lag][1]()

        if mode == "gqa":
            for kvh in range(4):
                cur = {}

                def pre_kv(kvh=kvh, cur=cur):
                    kt, rk = cur["k"]
                    vt, rv = cur["v"]
                    S.dma("sp", lambda q: q.dma_start(out=kt[:, 0:Sq], in_=self.QKT[si][36 + kvh]),
                          rk, reads=[self.r_QKT[si]], writes=[rk])
                    vsrc = self.VTM[si][:, 1536 + kvh * 128:1536 + (kvh + 1) * 128].rearrange("(n p) d -> p n d", p=128)
                    S.dma("sp", lambda q: q.dma_start(out=vt[:, 0:nkb, 0:128], in_=vsrc),
                          rv, reads=[self.r_VTM[si]], swrites=[rv])

                cur["k"] = L["k"].next()
                cur["v"] = L["v"].next()
                kt, rk = cur["k"]
                vt, rv = cur["v"]
                for g in range(3):
                    h = kvh * 3 + g
                    for qt in range(nqt):
                        def fin(banks, h=h, qt=qt):
                            ost, ro = L["ostg"].next()
                            for qs in range(4):
                                bk, rbk = banks[qs]
                                S.op("dve", lambda v, bk=bk, qs=qs: v.reciprocal(out=L["rden"][:, qs:qs + 1], in_=bk[:, 128:129]),
                                     reads=[rbk], writes=[L["r_rden"]])
                                S.op("dve", lambda v, bk=bk, qs=qs, ost=ost: v.tensor_scalar(
                                    out=ost[:, qs, 0:128], in0=bk[:, 0:128], scalar1=L["rden"][:, qs:qs + 1], scalar2=None,
                                    op0=ALU.mult), reads=[rbk, L["r_rden"]],
                                    writes=([ro] if qs == 0 else ()), swrites=(() if qs == 0 else [ro]))
                            dst = AOs[qt * 512:(qt + 1) * 512, 512 + h * 128:512 + (h + 1) * 128].rearrange(
                                "(s p) d -> p s d", p=128)
                            S.dma("sp", lambda q, ost=ost, dst=dst: q.dma_start(out=dst, in_=ost[:, :, 0:128]),
                                  ro, reads=[ro], swrites=[self.r_AO[si]])
                        qsrc = self.QKT[si][24 + h, :, qt * 512:(qt + 1) * 512]
                        add_tile(pre_kv if (g == 0 and qt == 0) else None, kt, rk, qsrc, vt, rv, qt, h, fin)
            run_pipeline(2)
            return
        for h in range(4):
            vt, rv = L["v"].next()
            ks = [L["k"].next() for _ in range(2)]

            def pre_h(h=h, vt=vt, rv=rv, ks=ks):
                vsrc = self.VTM[si][:, h * 256:(h + 1) * 256].rearrange("(n p) d -> p n d", p=128)
                S.dma("sp", lambda q: q.dma_start(out=vt[:, 0:nkb, 0:256], in_=vsrc),
                      rv, reads=[self.r_VTM[si]], swrites=[rv])
                for c in range(2):
                    kt, rk = ks[c]
                    S.dma("sp", lambda q, kt=kt, c=c: q.dma_start(out=kt[:, 0:Sq], in_=self.QKT[si][8 + 2 * h + c]),
                          rk, reads=[self.r_QKT[si]], writes=[rk])

            for qt in range(nqt):
                hold = {}
                for c in range(2):
                    def fin(banks, h=h, qt=qt, c=c, hold=hold):
                        if c == 0:
                            hold["ost"] = L["ostg"].next()
                        ost, ro = hold["ost"]
                        for qs in range(4):
                            bk, rbk = banks[qs]
                            S.op("dve", lambda v, bk=bk, qs=qs: v.reciprocal(out=L["rden"][:, qs:qs + 1], in_=bk[:, 256:257]),
                                 reads=[rbk], writes=[L["r_rden"]])
                            if c == 0:
                                S.op("act", lambda a, bk=bk, qs=qs: a.activation(
                                    out=L["o1"][:, qs, :], in_=bk[:, 0:256], func=AF.Copy, scale=L["rden"][:, qs:qs + 1]),
                                    reads=[rbk, L["r_rden"]], writes=[L["r_o1"]] if qs == 0 else (),
                                    swrites=() if qs == 0 else [L["r_o1"]])
                            else:
                                S.op("dve", lambda v, qs=qs: v.tensor_tensor(
                                    out=L["nlr"][:, qs:qs + 1], in0=L["rden"][:, qs:qs + 1], in1=L["nlam"][:], op=ALU.mult),
                                    reads=[L["r_rden"], L["r_nlam"]], writes=[L["r_nlr"]])
                                S.op("dve", lambda v, bk=bk, qs=qs: v.scalar_tensor_tensor(
                                    out=L["of"][:], in0=bk[:, 0:256], scalar=L["nlr"][:, qs:qs + 1], in1=L["o1"][:, qs, :],
                                    op0=ALU.mult, op1=ALU.add), reads=[rbk, L["r_nlr"], L["r_o1"]], writes=[L["r_of"]])
                                S.op("act", lambda a: a.activation(out=L["junk"][:], in_=L["of"][:], func=AF.Square,
                                                                   accum_out=L["ssq"][:]),
                                     reads=[L["r_of"]], writes=[L["r_junk"], L["r_ssq"]])
                                S.op("act", lambda a: a.activation(out=L["lnq"][:], in_=L["ssq"][:], func=AF.Ln,
                                                                   scale=1.0 / 256, bias=self.eps_ap()),
                                     reads=[L["r_ssq"], self.r_const], writes=[L["r_lnq"]])
                                S.op("act", lambda a: a.activation(out=L["rsq"][:], in_=L["lnq"][:], func=AF.Exp, scale=-0.5),
                                     reads=[L["r_lnq"]], writes=[L["r_rsq"]])
                                S.op("dve", lambda v, qs=qs, ost=ost: v.scalar_tensor_tensor(
                                    out=ost[:, qs, :], in0=L["of"][:], scalar=L["rsq"][:, 0:1], in1=L["gsub"][:],
                                    op0=ALU.mult, op1=ALU.mult), reads=[L["r_of"], L["r_rsq"], rc],
                                    writes=([ro] if qs == 0 else ()), swrites=(() if qs == 0 else [ro]))
                        if c == 1:
                            dst = AOs[qt * 512:(qt + 1) * 512, h * 256:(h + 1) * 256].rearrange("(s p) d -> p s d", p=128)
                            S.dma("sp", lambda q, ost=ost, dst=dst: q.dma_start(out=dst, in_=ost[:]),
                                  ro, reads=[ro], swrites=[self.r_AO[si]])
                    qsrc = self.QKT[si][2 * h + c, :, qt * 512:(qt + 1) * 512]
                    add_tile(pre_h if (qt == 0 and c == 0) else None, ks[c][0], ks[c][1], qsrc, vt, rv, qt, h, fin)
        run_pipeline(2)

    def _mk_na(self, L, A):
        S, RL = self.S, L["_res"]
        L["q"] = Ring(S, A, "nq", 2, [128, 4096], BF16, dma=True, reslist=RL)
        L["k"] = Ring(S, A, "nk", 2, [128, 4096], BF16, dma=True, reslist=RL)
        L["ve"] = Ring(S, A, "nve", 2, [128, 32, 129], BF16, dma=True, reslist=RL)
        L["vo"] = Ring(S, A, "nvo", 2, [128, 32, 129], BF16, dma=True, reslist=RL)
        L["bh"] = Ring(S, A, "nbh", 2, [128, 14, 64], F32, dma=True, reslist=RL)
        L["x"] = Ring(S, A, "nx", 3, [128, 4, 64], F32, reslist=RL)
        L["pt"] = Ring(S, A, "npt", 4, [128, 4, 64], BF16, reslist=RL)
        L["ostg"] = Ring(S, A, "nost", 2, [64, 8, 128], BF16, dma=True, reslist=RL)
        L["cm"] = A.alloc("ncm", [128, 64], F32)
        L["r_c"] = S.res("nac", dma=True)
        L["rden"] = A.alloc("nrden", [64, 1], F32)
        L["r_rden"] = S.res("nrden")
        RL += [L["r_c"], L["r_rden"]]

    def na_attn(self, si):
        S = self.S
        Sq = self.seqs[si]
        R = Sq // 64
        nb = Sq // 128
        L = self.use_layout("na", self._mk_na)
        rc = L["r_c"]
        S.dma("sp", lambda q: q.dma_start(out=L["cm"][:], in_=self.w["c_cm"]), rc, writes=[rc])
        for ring in (L["ve"], L["vo"]):
            for (vt, rv) in ring.slots:
                S.op("dve", lambda v, vt=vt: v.memset(vt[:, :, 128:129], 1.0), writes=[rv])
        cnt = {"acc": 0, "sc": 0}
        units = []
        for h in range(8):
            hs = {}

            def loads(h=h, hs=hs):
                qt_, rq = L["q"].next()
                S.dma("sp", lambda q: q.dma_start(out=qt_[:, 0:Sq], in_=self.QKT[si][16 + h]),
                      rq, reads=[self.r_QKT[si]], writes=[rq])
                kt, rk = L["k"].next()
                S.dma("sp", lambda q: q.dma_start(out=kt[:, 0:Sq], in_=self.QKT[si][24 + h]),
                      rk, reads=[self.r_QKT[si]], writes=[rk])
                ve, rve = L["ve"].next()
                vo, rvo = L["vo"].next()
                c0 = 1024 + h * 128
                se = self.VTM[si][:, c0:c0 + 128].rearrange("(n p) d -> p n d", p=128)
                so = self.VTM[si][64:Sq - 64, c0:c0 + 128].rearrange("(n p) d -> p n d", p=128)
                S.dma("sp", lambda q: q.dma_start(out=ve[:, 0:nb, 0:128], in_=se),
                      rve, reads=[self.r_VTM[si]], swrites=[rve])
                S.dma("sp", lambda q: q.dma_start(out=vo[:, 0:nb - 1, 0:128], in_=so),
                      rvo, reads=[self.r_VTM[si]], swrites=[rvo])
                bh, rbh = L["bh"].next()
                S.dma("sp", lambda q: q.dma_start(out=bh[:].rearrange("p a b -> p (a b)"), in_=self.w["rpbx"][h]),
                      rbh, writes=[rbh])
                S.op("dve", lambda v: v.tensor_tensor(
                    out=bh[:], in0=bh[:], in1=L["cm"][:].unsqueeze(1).broadcast_to([128, 14, 64]), op=ALU.add),
                    reads=[rc, rbh], writes=[rbh])
                hs.update(q=(qt_, rq), k=(kt, rk), ve=(ve, rve), vo=(vo, rvo), bh=(bh, rbh), c0=c0, ost=None)

            pre_at = max(0, len(units) - 12)
            for qr in range(R):
                def mk(qr=qr, hs=hs, first=(qr == 0), loads=loads):
                    hold = {}
                    r0 = min(max(qr - 4, 0), R - 8)
                    rho = r0 - qr + 7

                    def front():
                        qt_, rq = hs["q"]
                        kt, rk = hs["k"]
                        bh, rbh = hs["bh"]
                        sc, rsc = self.ps[cnt["sc"] % 2], self.r_ps[cnt["sc"] % 2]
                        cnt["sc"] += 1
                        for i in range(4):
                            ks = (r0 + 2 * i) * 64
                            S.op("pe", lambda t, ks=ks, i=i: t.matmul(
                                sc[:, i * 64:(i + 1) * 64], lhsT=kt[:, ks:ks + 128], rhs=qt_[:, qr * 64:(qr + 1) * 64],
                                start=True, stop=True), reads=[rk, rq], writes=[rsc], signal=(i == 3))
                        xt, rx = L["x"].next()
                        S.op("dve", lambda v: v.scalar_tensor_tensor(
                            out=xt[:], in0=sc[:, 0:256].rearrange("p (a b) -> p a b", a=4), scalar=SCALE,
                            in1=bh[:, rho:rho + 7:2, :], op0=ALU.mult, op1=ALU.add), reads=[rsc, rbh], writes=[rx])
                        pt, rpt = L["pt"].next()
                        S.op("act", lambda a: a.activation(out=pt[:], in_=xt[:], func=AF.Exp),
                             reads=[rx], writes=[rpt])
                        hold["pt"] = (pt, rpt)

                    def back():
                        pt, rpt = hold["pt"]
                        ve, rve = hs["ve"]
                        vo, rvo = hs["vo"]
                        ai = 2 + cnt["acc"] % 6
                        cnt["acc"] += 1
                        bk, rbk = self.ps[ai], self.r_ps[ai]
                        for i in range(4):
                            row = r0 + 2 * i
                            vsrc, rvs = (ve, rve) if row % 2 == 0 else (vo, rvo)
                            vb = row // 2
                            S.op("pe", lambda t, i=i, vsrc=vsrc, vb=vb: t.matmul(
                                bk[0:64, 0:129], lhsT=pt[:, i, :], rhs=vsrc[:, vb, :], start=(i == 0), stop=(i == 3)),
                                reads=[rpt, rvs], writes=[rbk], signal=(i == 3))
                        if qr % 8 == 0:
                            hs["ost"] = L["ostg"].next()
                        ost, ro = hs["ost"]
                        S.op("dve", lambda v: v.reciprocal(out=L["rden"][:], in_=bk[0:64, 128:129]),
                             reads=[rbk], writes=[L["r_rden"]])
                        fst = (qr % 8 == 0)
                        S.op("act", lambda a: a.activation(
                            out=ost[:, qr % 8, :], in_=bk[0:64, 0:128], func=AF.Copy, scale=L["rden"][:, 0:1]),
                            reads=[rbk, L["r_rden"]], writes=([ro] if fst else ()), swrites=(() if fst else [ro]))
                        if qr % 8 == 7:
                            q0 = (qr - 7) * 64
                            c0 = hs["c0"]
                            dst = self.AO[si][q0:q0 + 512, c0:c0 + 128].rearrange("(r p) d -> p r d", p=64)
                            S.dma("sp", lambda q: q.dma_start(out=dst, in_=ost[:]),
                                  ro, reads=[ro], swrites=[self.r_AO[si]])
                    return front, back
                units.append(list(mk()) + [[]])
            units[pre_at][2].append(loads)
        self.run_units(units, 2)

    def run_units(self, units, lag):
        n = len(units)
        for i in range(n + lag):
            if i < n:
                for p_ in units[i][2]:
                    p_()
                units[i][0]()
            if i - lag >= 0:
                units[i - lag][1]()

    def _mk_dil(self, L, A):
        S, RL = self.S, L["_res"]
        L["q"] = Ring(S, A, "lq", 2, [128, 4096], BF16, dma=True, reslist=RL)
        L["kp"] = Ring(S, A, "lkp", 2, [128, 6144], BF16, dma=True, reslist=RL)
        L["vp"] = Ring(S, A, "lvp", 2, [128, 48, 129], BF16, dma=True, reslist=RL)
        L["bd"] = Ring(S, A, "lbd", 2, [128, 256], F32, dma=True, reslist=RL)
        L["x"] = Ring(S, A, "lx", 3, [128, 256], F32, reslist=RL)
        L["pt"] = Ring(S, A, "lpt", 4, [128, 256], BF16, reslist=RL)
        L["cst"] = Ring(S, A, "lcst", 3, [128, 4, 132], F32, dma=True, reslist=RL)
        L["cnin"] = Ring(S, A, "lcn", 2, [128, 3, 4, 132], F32, dma=True, reslist=RL)
        L["ocs"] = Ring(S, A, "locs", 2, [128, 4, 128], BF16, dma=True, reslist=RL)
        L["s1"] = A.alloc("ls1", [128, 4, 132], F32)
        L["r_s1"] = S.res("ls1")
        L["rden"] = A.alloc("lrden", [128, 4], F32)
        L["r_rden"] = S.res("lrden")
        RL += [L["r_s1"], L["r_rden"]]

    def dil_attn(self, si):
        S = self.S
        Sq = self.seqs[si]
        L = self.use_layout("dil", self._mk_dil)
        for (kp, rkp) in L["kp"].slots:
            S.op("dve", lambda v, kp=kp: v.memset(kp[:], 0.0), writes=[rkp])
        cnt = {"acc": 0, "sc": 0}
        units = []
        for g, dil in enumerate((1, 4, 16)):
            Lg = Sq // dil
            nb = Lg // 128
            for hh in range(4):
                head = g * 4 + hh
                hs = {}

                def loads(head=head, hs=hs, dil=dil, nb=nb, Lg=Lg):
                    qt_, rq = L["q"].next()
                    S.dma("sp", lambda q: q.dma_start(out=qt_[:, 0:Sq], in_=self.QKT[si][head]),
                          rq, reads=[self.r_QKT[si]], writes=[rq])
                    kp, rkp = L["kp"].next()
                    S.dma("sp", lambda q: q.dma_start(out=kp[:, 1024:1024 + Sq], in_=self.QKT[si][12 + head]),
                          rkp, reads=[self.r_QKT[si]], swrites=[rkp])
                    vp, rvp = L["vp"].next()
                    nblk = dil * (nb + 1)
                    vp4 = vp[:, 0:nblk, :].rearrange("p (r b) d -> p r b d", r=dil)
                    S.op("dve", lambda v: v.memset(vp[:, 0:nblk, :], 0.0), writes=[rvp])
                    S.op("dve", lambda v: v.memset(vp4[64:128, :, 0, 128:129], 1.0), swrites=[rvp])
                    if nb > 1:
                        S.op("dve", lambda v: v.memset(vp4[:, :, 1:nb, 128:129], 1.0), swrites=[rvp])
                    S.op("dve", lambda v: v.memset(vp4[0:64, :, nb, 128:129], 1.0), swrites=[rvp])
                    vs = self.VTM[si][:, head * 128:(head + 1) * 128].rearrange("(m r) d -> m r d", r=dil)
                    S.dma("sp", lambda q: q.dma_start(out=vp4[64:128, :, 0, 0:128], in_=vs[0:64]),
                          rvp, reads=[self.r_VTM[si]], swrites=[rvp])
                    if nb > 1:
                        for r in range(dil):
                            srcb = vs[64:64 + (nb - 1) * 128, r, :].rearrange("(b p) d -> p b d", p=128)
                            S.dma("sp", lambda q, srcb=srcb, r=r: q.dma_start(
                                out=vp4[:, r, 1:nb, 0:128], in_=srcb), rvp, reads=[self.r_VTM[si]], swrites=[rvp])
                    S.dma("sp", lambda q: q.dma_start(
                        out=vp4[0:64, :, nb, 0:128], in_=vs[Lg - 64:Lg]), rvp, reads=[self.r_VTM[si]], swrites=[rvp])
                    bd, rbd = L["bd"].next()
                    S.dma("sp", lambda q: q.dma_start(
                        out=bd[:], in_=self.w["c_bd"][:, head * 256:(head + 1) * 256]), rbd, writes=[rbd])
                    hs.update(q=(qt_, rq), kp=(kp, rkp), vp4=vp4, rvp=rvp, bd=(bd, rbd))

                pre_at = max(0, len(units) - 12)
                cnv = self.CN[si][g, :, hh, 0:129].rearrange("(m r) d -> m r d", r=dil)
                for r in range(dil):
                    for n in range(nb):
                        def mk(r=r, n=n, hs=hs, dil=dil, cnv=cnv, nb=nb):
                            hold = {}

                            def front():
                                qt_, rq = hs["q"]
                                kp, rkp = hs["kp"]
                                bd, rbd = hs["bd"]
                                sc, rsc = self.ps[cnt["sc"] % 2], self.r_ps[cnt["sc"] % 2]
                                cnt["sc"] += 1
                                q0 = 128 * n * dil + r
                                qsl = qt_[:, q0:q0 + 127 * dil + 1:dil] if dil > 1 else qt_[:, q0:q0 + 128]
                                for j in range(2):
                                    k0 = 1024 + (128 * (n + j) - 64) * dil + r
                                    ksl = kp[:, k0:k0 + 127 * dil + 1:dil] if dil > 1 else kp[:, k0:k0 + 128]
                                    S.op("pe", lambda t, ksl=ksl, j=j: t.matmul(
                                        sc[:, j * 128:(j + 1) * 128], lhsT=ksl, rhs=qsl, start=True, stop=True),
                                        reads=[rkp, rq], writes=[rsc], signal=(j == 1))
                                xt, rx = L["x"].next()
                                S.op("dve", lambda v: v.scalar_tensor_tensor(
                                    out=xt[:], in0=sc[:, 0:256], scalar=SCALE, in1=bd[:], op0=ALU.mult, op1=ALU.add),
                                    reads=[rsc, rbd], writes=[rx])
                                pt, rpt = L["pt"].next()
                                S.op("act", lambda a: a.activation(out=pt[:], in_=xt[:], func=AF.Exp),
                                     reads=[rx], writes=[rpt])
                                hold["pt"] = (pt, rpt)

                            def back():
                                pt, rpt = hold["pt"]
                                vp4, rvp = hs["vp4"], hs["rvp"]
                                ai = 2 + cnt["acc"] % 6
                                cnt["acc"] += 1
                                bk, rbk = self.ps[ai], self.r_ps[ai]
                                for j in range(2):
                                    S.op("pe", lambda t, j=j: t.matmul(
                                        bk[:, 0:129], lhsT=pt[:, j * 128:(j + 1) * 128], rhs=vp4[:, r, n + j, :],
                                        start=(j == 0), stop=(j == 1)), reads=[rpt, rvp], writes=[rbk], signal=(j == 1))
                                nbat = min(4, nb)
                                if n % nbat == 0:
                                    hs["cst"] = L["cst"].next()
                                cst, rcs = hs["cst"]
                                fst = (n % nbat == 0)
                                self.copy(self.ev_eng(), cst[:, n % nbat, 0:129], bk[:, 0:129], reads=[rbk],
                                          writes=([rcs] if fst else ()), swrites=(() if fst else [rcs]))
                                if n % nbat == nbat - 1:
                                    n0 = n - (nbat - 1)
                                    dst = cnv[128 * n0:128 * (n0 + nbat), r, :].rearrange("(b p) d -> p b d", p=128)
                                    S.dma("sp", lambda q: q.dma_start(out=dst, in_=cst[:, 0:nbat, 0:129]),
                                          rcs, reads=[rcs], swrites=[self.r_CN[si]])
                            return front, back
                        units.append(list(mk()) + [[]])
                units[pre_at][2].append(loads)
        self.run_units(units, 2)
        for blk in range(Sq // 128):
            cn, rcn = L["cnin"].next()
            src = self.CN[si][:, blk * 128:(blk + 1) * 128, :, :].rearrange("g p h d -> p g h d")
            S.dma("sp", lambda q, cn=cn, src=src: q.dma_start(out=cn[:], in_=src), rcn,
                  reads=[self.r_CN[si]], writes=[rcn])
            s1 = L["s1"]
            S.op("dve", lambda v, cn=cn: v.tensor_tensor(out=s1[:, :, 0:129], in0=cn[:, 0, :, 0:129],
                                                         in1=cn[:, 1, :, 0:129], op=ALU.add),
                 reads=[rcn], writes=[L["r_s1"]])
            S.op("dve", lambda v, cn=cn: v.tensor_tensor(out=s1[:, :, 0:129], in0=s1[:, :, 0:129],
                                                         in1=cn[:, 2, :, 0:129], op=ALU.add),
                 reads=[rcn, L["r_s1"]], writes=[L["r_s1"]])
            S.op("dve", lambda v: v.reciprocal(out=L["rden"][:].unsqueeze(2), in_=s1[:, :, 128:129]),
                 reads=[L["r_s1"]], writes=[L["r_rden"]])
            oc, roc = L["ocs"].next()
            S.op("dve", lambda v, oc=oc: v.tensor_tensor(
                out=oc[:], in0=s1[:, :, 0:128], in1=L["rden"][:].unsqueeze(2).broadcast_to([128, 4, 128]),
                op=ALU.mult), reads=[L["r_s1"], L["r_rden"]], writes=[roc])
            S.dma("sp", lambda q, oc=oc, blk=blk: q.dma_start(
                out=self.AO[si][blk * 128:(blk + 1) * 128, 0:512], in_=oc[:].rearrange("p h d -> p (h d)")),
                roc, reads=[roc], swrites=[self.r_AO[si]])

    def build(self):
        S = self.S
        self.load_consts()
        nseq = len(self.seqs)
        ph = self.dbg.get("phases", "abcd")
        if "i" not in ph:
            for si in range(nseq):
                self.phase_in(si)
        groups = [(layer, si) for layer in self.layers for si in range(nseq)]
        for gi_, (layer, si) in enumerate(groups):
            tiles = [self.tokbase[si] // TA + i for i in range(self.seqs[si] // TA)]
            if "a" in ph:
                for t in tiles:
                    self.phase_a(layer, si, t)
            if "b" in ph:
                if layer % 2 == 0:
                    if "1" not in ph:
                        self.dense_attn(si, "diff", layer)
                    if "2" not in ph:
                        self.na_attn(si)
                else:
                    if "1" not in ph:
                        self.dil_attn(si)
                    if "2" not in ph:
                        self.dense_attn(si, "gqa", layer)
            if "c" in ph:
                for t in tiles:
                    self.phase_c(layer, si, t)
            if "d" in ph:
                for k, t in enumerate(tiles):
                    nxt = None
                    if k + 1 < len(tiles):
                        nxt = ("d", layer, tiles[k + 1])
                    elif gi_ + 1 < len(groups) and "a" in ph and ph == "abcd":
                        nl, ns = groups[gi_ + 1]
                        nxt = ("a", nl, self.tokbase[ns] // TA)
                    self.phase_d(layer, t, nxt)
        if "o" not in ph:
            for si in range(nseq):
                self.phase_out(si)
        S.barrier([self.r_y], engines=("sp",))
        S.emit()
        return self.nc


_CACHE = {}


def _get_nc(seqs):
    key = tuple(seqs)
    if key not in _CACHE:
        _CACHE[key] = Builder(list(seqs)).build()
    return _CACHE[key]


def _c_ab():
    ab = np.zeros((128, 4, 32), np.float32)
    sl = _alibi(4)
    for h in range(4):
        ab[:, h, :] = -sl[h] * 128.0 * np.arange(32, dtype=np.float32)[None, :]
    return ab.reshape(128, 128)


def make_in_maps(inputs, cores, seq_names=("x_prompt", "x_sample")):
    f = lambda a: np.ascontiguousarray(np.asarray(a, dtype=np.float32))
    shared = {
        "attn_norm_g": f(inputs["attn_norm_g"]), "ev_w_in": f(inputs["ev_w_in"])[0],
        "ev_lambda": f(inputs["ev_lambda"])[0], "ev_subln_g": f(inputs["ev_subln_g"]),
        "rpbx": rpb_layout(inputs["ev_rpb"]).reshape(8, 128, 14 * 64),
        "ev_w_out": f(inputs["ev_w_out"])[0], "od_w_in": f(inputs["od_w_in"])[0],
        "od_qk_norm_g": f(inputs["od_qk_norm_g"])[0], "od_w_out": f(inputs["od_w_out"])[0],
        "ffn_norm_g": f(inputs["ffn_norm_g"]), "ffn_w_gate": f(inputs["ffn_w_gate"]),
        "ffn_w_up": f(inputs["ffn_w_up"]), "ffn_w_down": f(inputs["ffn_w_down"]),
        "final_norm_g": f(inputs["final_norm_g"]).reshape(1, D),
    }
    hc = host_consts()
    shared["c_ident"] = hc["c_ident"]
    shared["c_dq"] = hc["c_dq"].reshape(128, 5 * 512)
    shared["c_bd"] = hc["c_bd"].reshape(128, 12 * 256)
    shared["c_cm"] = hc["c_cm"]
    shared["c_cos"] = hc["c_cos"]
    shared["c_sin"] = hc["c_sin"]
    shared["c_ab"] = _c_ab()
    maps = []
    for b in cores:
        m = dict(shared)
        for i, n in enumerate(seq_names):
            m[f"x{i}"] = f(inputs[n][b])
        maps.append(m)
    return maps


def kernel(**inputs):
    n = 8
    seqs = (inputs["x_prompt"].shape[1], inputs["x_sample"].shape[1])
    nc = _get_nc(seqs)
    in_maps = make_in_maps(inputs, list(range(n)))
    res = run_bass_kernel_spmd(nc, in_maps, core_ids=list(range(n)))
    yp = np.stack([np.asarray(res.results[b]["y0"], dtype=np.float32) for b in range(n)], axis=0)
    ys = np.stack([np.asarray(res.results[b]["y1"], dtype=np.float32) for b in range(n)], axis=0)
    return (yp, ys)
```

```python
import math
import numpy as np
import ml_dtypes
import concourse.bass as bass
import concourse.mybir as mybir
from concourse.bass_utils import run_bass_kernel_spmd

F32 = mybir.dt.float32
BF16 = mybir.dt.bfloat16
AF = mybir.ActivationFunctionType
ALU = mybir.AluOpType
AX = mybir.AxisListType

D = 2048
KC = 16
HD = 128
EPS = 1e-6
SCALE = HD ** -0.5
NEG = -30000.0
FFN = 5632
FC = 44
TA = 1024
SAME_ENGINE_SYNC = True


class Res:
    __slots__ = ("name", "lw", "ls", "rd", "dsem", "dcnt")

    def __init__(self, name):
        self.name = name
        self.lw = {}
        self.ls = {}
        self.rd = {}
        self.dsem = None
        self.dcnt = 0


class Sched:
    ENGS = ("pe", "act", "dve", "pool", "sp")

    def __init__(self, nc):
        self.nc = nc
        self.sem = {e: nc.alloc_semaphore("prog_" + e) for e in ("pe", "act", "dve", "pool")}
        self.nsig = {e: 0 for e in self.sem}
        self.waited = {e: {} for e in self.ENGS}
        self.prog = {e: [] for e in self.ENGS}
        self.engsem = {id(s): e for e, s in self.sem.items()}
        self.nres = 0
        self.nsem = 4

    def res(self, name=None, dma=False):
        self.nres += 1
        r = Res(name or f"r{self.nres}")
        if dma:
            r.dsem = self.nc.alloc_semaphore(f"d{self.nres}_" + r.name)
            self.nsem += 1
        return r

    def _collect(self, eng, reads, writes, swrites=()):
        deps = {}

        def need(t):
            s, v = t
            k = id(s)
            if k not in deps or deps[k][1] < v:
                deps[k] = (s, v)

        for r in reads:
            for t in r.lw.values():
                need(t)
            for t in r.ls.values():
                need(t)
        for w in writes:
            for t in w.lw.values():
                need(t)
            for t in w.ls.values():
                need(t)
            for t in w.rd.values():
                need(t)
        for w in swrites:
            for t in w.lw.values():
                need(t)
            for t in w.rd.values():
                need(t)
        waits = []
        wd = self.waited[eng]
        for k, (s, v) in deps.items():
            if wd.get(k, 0) >= v:
                continue
            src = self.engsem.get(k)
            if src is not None:
                if src == eng and (eng == "pe" or not SAME_ENGINE_SYNC):
                    continue
                if v > self.nsig[src]:
                    raise RuntimeError(f"unsignaled dep: {eng} waits {src}@{v} nsig={self.nsig[src]}")
            wd[k] = v
            waits.append((s, v))
        return waits

    def _mark(self, tk, reads, writes, swrites):
        s, v = tk
        k = id(s)
        for r in reads:
            r.rd[k] = tk
        for w in writes:
            w.lw = {k: tk}
            w.ls = {}
            w.rd = {}
        for w in swrites:
            w.ls[k] = tk

    def op(self, eng, fn, reads=(), writes=(), swrites=(), signal=True):
        waits = self._collect(eng, reads, writes, swrites)
        if signal:
            self.nsig[eng] += 1
            tk = (self.sem[eng], self.nsig[eng])
        else:
            tk = (self.sem[eng], self.nsig[eng] + 1)
        self.prog[eng].append((waits, fn, (self.sem[eng], 1) if signal else None))
        self._mark(tk, reads, writes, swrites)
        return tk

    def dma(self, q, fn, slot, reads=(), writes=(), swrites=()):
        waits = self._collect(q, reads, writes, swrites)
        slot.dcnt += 16
        tk = (slot.dsem, slot.dcnt)
        self.prog[q].append((waits, fn, (slot.dsem, 16)))
        self._mark(tk, reads, writes, swrites)
        return tk

    def barrier(self, resources, engines=None):
        for e in (engines or self.ENGS):
            waits = self._collect(e, (), resources)
            if waits:
                self.prog[e].append((waits, None, None))

    def emit(self):
        nc = self.nc
        prog = self.prog

        def run(engobj, lst):
            for waits, fn, sig in lst:
                for s, v in waits:
                    engobj.wait_ge(s, v)
                if fn is None:
                    continue
                ins = fn(engobj)
                if sig is not None:
                    ins.then_inc(sig[0], sig[1])

        with nc.Block() as block:
            @block.tensor
            def _(e):
                run(e, prog["pe"])

            @block.scalar
            def _(e):
                run(e, prog["act"])

            @block.vector
            def _(e):
                run(e, prog["dve"])

            @block.gpsimd
            def _(e):
                run(e, prog["pool"])

            @block.sync
            def _(e):
                run(e, prog["sp"])


def _dsize(dt):
    return 4 if dt == F32 else 2


class Arena:
    def __init__(self, nc, base, limit):
        self.nc, self.base, self.off, self.limit = nc, base, base, limit
        self.n = 0
        self.addr = {}

    def alloc(self, name, shape, dtype):
        size = int(np.prod(shape[1:])) * _dsize(dtype)
        size = (size + 63) // 64 * 64
        assert self.off + size <= self.limit, f"SBUF overflow at {name}: {self.off}+{size}>{self.limit}"
        self.n += 1
        t = self.nc.alloc_sbuf_tensor_at(f"{name}_{self.base}_{self.n}", list(shape), dtype, offset=self.off)
        self.addr[id(t)] = self.off
        self.off += size
        return t

    def reset(self):
        self.off = self.base


class Ring:
    def __init__(self, S, arena, name, n, shape, dtype, dma=False, reslist=None):
        self.slots = []
        for i in range(n):
            t = arena.alloc(f"{name}{i}", shape, dtype)
            r = S.res(f"{name}{i}", dma=dma)
            self.slots.append((t, r))
            if reslist is not None:
                reslist.append(r)
        self.i = 0

    def next(self):
        s = self.slots[self.i % len(self.slots)]
        self.i += 1
        return s


def _alibi(n):
    return [2.0 ** (-8.0 * (i + 1) / n) for i in range(n)]


def host_consts():
    c = {}
    c["c_ident"] = np.eye(128, dtype=np.float32)
    kl = np.arange(128)[:, None].astype(np.float32)
    ql = np.arange(512)[None, :].astype(np.float32)
    Dm = ql - kl
    dq = np.zeros((128, 5, 512), np.float32)
    dq[:, 0] = Dm
    for j, d0 in enumerate((0, -128, -256, -384)):
        dq[:, 1 + j] = np.abs(Dm + d0)
    c["c_dq"] = dq
    sl = _alibi(12)
    dils = (1, 4, 16)
    bd = np.zeros((128, 12, 2, 128), np.float32)
    k2 = np.arange(128)[:, None]
    q2 = np.arange(128)[None, :]
    for h in range(12):
        dil = dils[h // 4]
        rel0 = (k2 - q2 - 64).astype(np.float32)
        rel1 = (k2 - q2 + 64).astype(np.float32)
        bd[:, h, 0] = np.where(k2 >= q2, -sl[h] * dil * np.abs(rel0), NEG)
        bd[:, h, 1] = np.where(k2 <= q2, -sl[h] * dil * np.abs(rel1), NEG)
    c["c_bd"] = bd
    qc = np.arange(64)
    c0 = np.clip(qc - 8, 0, 48)
    kc = np.arange(64)
    valid = (kc[:, None] >= c0[None, :]) & (kc[:, None] < c0[None, :] + 16)
    cm = np.where(valid, 0.0, NEG).astype(np.float32)
    c["c_cm"] = np.concatenate([cm, cm], axis=0)
    t = np.arange(4096)
    row = (t // 64).astype(np.float32)
    col = (t % 64).astype(np.float32)
    f_row = (10000.0 ** (-np.arange(0, 64, 2, dtype=np.float32) / 64)).astype(np.float32)
    f_col = (10000.0 ** (-np.arange(0, 64, 2, dtype=np.float32) / 64)).astype(np.float32)
    ang = np.concatenate([row[:, None] * f_row[None, :], col[:, None] * f_col[None, :]], axis=-1).astype(np.float32)
    c["c_cos"] = np.cos(ang).astype(np.float32)
    c["c_sin"] = np.sin(ang).astype(np.float32)
    return c


def rpb_layout(rpb):
    rpb = np.asarray(rpb, np.float32)[0]
    kc = np.arange(64)[:, None]
    qc = np.arange(64)[None, :]
    idx = np.clip(15 + kc - qc, 0, 30)
    out = np.zeros((8, 128, 14, 64), np.float32)
    for krl in range(2):
        for r0 in range(14):
            out[:, krl * 64:(krl + 1) * 64, r0, :] = rpb[:, r0 + krl][:, idx]
    return out


class Builder:
    def __init__(self, seqs, layers=(0, 1), dbg=None):
        self.seqs = seqs
        self.layers = layers
        self.dbg = dbg or {}
        self.ntok = sum(seqs)
        self.tokbase = [sum(seqs[:i]) for i in range(len(seqs))]
        nc = self.nc = bass.Bass("TRN2", target_bir_lowering=False)
        self.S = Sched(nc)
        self._decl_io()
        self._alloc()

    def _decl_io(self):
        nc = self.nc
        inp = lambda n, sh: nc.dram_tensor(n, list(sh), F32, kind="ExternalInput").ap()
        self.x = [inp(f"x{i}", (s, D)) for i, s in enumerate(self.seqs)]
        self.y = [nc.dram_tensor(f"y{i}", [s, D], F32, kind="ExternalOutput").ap() for i, s in enumerate(self.seqs)]
        self.w = {}
        for n, sh in (("attn_norm_g", (2, D)), ("ev_w_in", (D, 6144)), ("ev_lambda", (4, 128)),
                      ("ev_subln_g", (1, 256)), ("rpbx", (8, 128, 14 * 64)), ("ev_w_out", (D, D)),
                      ("od_w_in", (D, 7168)), ("od_qk_norm_g", (2, 128)), ("od_w_out", (D, D)),
                      ("ffn_norm_g", (2, D)), ("ffn_w_gate", (2, D, FFN)), ("ffn_w_up", (2, D, FFN)),
                      ("ffn_w_down", (2, FFN, D)), ("final_norm_g", (1, D)),
                      ("c_ident", (128, 128)), ("c_dq", (128, 5 * 512)), ("c_bd", (128, 12 * 256)),
                      ("c_cm", (128, 64)), ("c_cos", (4096, 64)), ("c_sin", (4096, 64)), ("c_ab", (128, 128))):
            self.w[n] = inp(n, sh)
        nt = self.ntok
        kd = "ExternalOutput" if self.dbg.get("dump") else "Internal"
        self.HT = nc.dram_tensor("HT", [KC, 128, nt], F32, kind=kd).ap()
        self.QKT = [nc.dram_tensor(f"QKT{i}", [40, 128, s], BF16, kind=kd).ap() for i, s in enumerate(self.seqs)]
        self.VTM = [nc.dram_tensor(f"VTM{i}", [s, 2048], BF16, kind=kd).ap() for i, s in enumerate(self.seqs)]
        self.AO = [nc.dram_tensor(f"AO{i}", [s, 2048], BF16, kind=kd).ap() for i, s in enumerate(self.seqs)]
        self.CN = [nc.dram_tensor(f"CN{i}", [3, s, 4, 132], F32).ap() for i, s in enumerate(self.seqs)]
        S = self.S
        self.r_HT = [[S.res(f"HT{t}_{c}") for c in range(KC)] for t in range(nt // 512)]
        self.r_QKT = [S.res(f"QKT{i}") for i in range(len(self.seqs))]
        self.r_VTM = [S.res(f"VTM{i}") for i in range(len(self.seqs))]
        self.r_AO = [S.res(f"AO{i}") for i in range(len(self.seqs))]
        self.r_CN = [S.res(f"CN{i}") for i in range(len(self.seqs))]
        self.r_y = S.res("yout")

    def _alloc(self):
        nc, S = self.nc, self.S
        SB0 = 16640
        self.DYN0 = SB0 + 102 * 1024
        self.DYN1 = 229376
        A = self.A = Arena(nc, SB0, self.DYN0)
        self.ident_f = A.alloc("identf", [128, 128], F32)
        self.ident_b = A.alloc("identb", [128, 128], BF16)
        self.ones_b = A.alloc("onesb", [128, 128], BF16)
        self.gT = A.alloc("gT", [128, 5, KC], F32)
        self.epsT = A.alloc("epsT", [128, 1], F32)
        self.r_const = S.res("const", dma=True)
        self.r_const2 = S.res("const2", dma=True)
        self.hring = Ring(S, A, "hT", 6, [128, 512], F32, dma=True)
        self.actT = A.alloc("actT", [128, KC, TA], BF16)
        self.r_actT = [S.res("actT0"), S.res("actT1")]
        self.sqring = Ring(S, A, "sq", 2, [128, 512], BF16)
        self.lnt = A.alloc("lnt", [128, 512], F32)
        self.r_lnt = S.res("lnt")
        self.rstd = A.alloc("rstd", [128, 512], F32)
        self.r_rstd = S.res("rstd")
        self.wring = Ring(S, A, "w", 2, [128, KC, 512], BF16, dma=True)
        self.wd_views = []
        for (t, r) in self.wring.slots:
            off = A.addr[id(t)]
            self.wd_views.append(nc.alloc_sbuf_tensor_at(f"wdv{len(self.wd_views)}", [128, FC, 128], BF16, offset=off))
        self.stg_b = Ring(S, A, "stgb", 2, [128, 1024], BF16, dma=True)
        self.stg_f = Ring(S, A, "stgf", 3, [128, 512], F32, dma=True)
        self.aoin = Ring(S, A, "aoin", 2, [128, 2048], BF16, dma=True)
        self.static_end = A.off
        self.DA = Arena(nc, self.DYN0, self.DYN1)
        self.layouts = {}
        self.cur_layout = None
        self.ps = [nc.alloc_psum_tensor(f"ps{i}", [128, 512], F32) for i in range(8)]
        self.r_ps = [S.res(f"ps{i}") for i in range(8)]
        self.psbs = [self.ps[6], self.ps[7]]
        self.r_psb = [self.r_ps[6], self.r_ps[7]]
        self.psi = 0
        self.evi = 0
        self.normed = None
        self.rope_pending = []

    def use_layout(self, name, maker):
        if self.cur_layout == name:
            return self.layouts[name]
        self.layout_epoch = getattr(self, "layout_epoch", 0) + 1
        if self.cur_layout is not None:
            self.S.barrier(self.layouts[self.cur_layout]["_res"])
        if name not in self.layouts:
            self.DA.reset()
            L = {"_res": []}
            maker(L, self.DA)
            self.layouts[name] = L
        self.cur_layout = name
        return self.layouts[name]

    def gemm_bank(self):
        i = self.psi % 4
        self.psi += 1
        return self.ps[i], self.r_ps[i]

    def ev_eng(self):
        self.evi += 1
        return "act" if self.evi % 2 else "dve"

    def copy(self, eng, out, in_, reads, writes=(), swrites=()):
        if eng == "act":
            return self.S.op("act", lambda a: a.copy(out=out, in_=in_), reads=reads, writes=writes, swrites=swrites)
        return self.S.op(eng, lambda v: v.tensor_copy(out=out, in_=in_), reads=reads, writes=writes, swrites=swrites)

    def load_consts(self):
        S, w = self.S, self.w
        rc = self.r_const
        S.dma("sp", lambda q: q.dma_start(out=self.ident_f[:], in_=w["c_ident"]), rc, swrites=[rc])
        S.dma("pool", lambda q: q.dma_start(out=self.ident_b[:], in_=w["c_ident"]), self.r_const2, writes=[self.r_const2])
        S.op("dve", lambda v: v.memset(self.ones_b[:], 1.0), swrites=[rc])
        S.op("dve", lambda v: v.memset(self.epsT[:], EPS), swrites=[rc])
        for i, (n, row) in enumerate((("attn_norm_g", 0), ("attn_norm_g", 1), ("ffn_norm_g", 0),
                                      ("ffn_norm_g", 1), ("final_norm_g", 0))):
            src = w[n][row:row + 1, :].rearrange("o (c p) -> p (o c)", p=128)
            S.dma("sp", lambda q, src=src, i=i: q.dma_start(out=self.gT[:, i, :], in_=src,
                                                            allow_slow_non_contiguous=True), rc, swrites=[rc])

    def phase_in(self, si):
        S = self.S
        Sq = self.seqs[si]
        for b in range(Sq // 128):
            g0 = self.tokbase[si] + b * 128
            tile = g0 // 512
            xin = self.use_layout("io", self._mk_io)["xin"]
            xt, rx = xin.next()
            S.dma("sp", lambda q, xt=xt, b=b: q.dma_start(out=xt[:], in_=self.x[si][b * 128:(b + 1) * 128, :]),
                  rx, writes=[rx])
            hst, rhs_ = self.use_layout("io", self._mk_io)["hst"].next()
            for cg in range(4):
                bank, rb = self.ps[4 + cg % 2], self.r_ps[4 + cg % 2]
                for j in range(4):
                    c = cg * 4 + j
                    S.op("pe", lambda t, bank=bank, j=j, c=c, xt=xt: t.transpose(
                        bank[:, j * 128:(j + 1) * 128], xt[:, c * 128:(c + 1) * 128], self.ident_f[:]),
                        reads=[rx, self.r_const], writes=[rb], signal=(j == 3))
                self.copy(self.ev_eng(), hst[:, cg * 4:(cg + 1) * 4, :], bank[:].rearrange("p (c t) -> p c t", c=4),
                          reads=[rb], writes=([rhs_] if cg == 0 else ()), swrites=(() if cg == 0 else [rhs_]))
            dst = self.HT[:, :, g0:g0 + 128].rearrange("c p t -> p c t")
            S.dma("sp", lambda q, hst=hst, dst=dst: q.dma_start(out=dst, in_=hst[:]), rhs_, reads=[rhs_],
                  swrites=[self.r_HT[tile][c] for c in range(KC)])

    def _mk_io(self, L, A):
        L["xin"] = Ring(self.S, A, "xin", 2, [128, 2048], F32, dma=True, reslist=L["_res"])
        L["ynT"] = Ring(self.S, A, "ynT", 2, [128, KC, 512], F32, reslist=L["_res"])
        L["hst"] = Ring(self.S, A, "hst", 2, [128, KC, 128], F32, dma=True, reslist=L["_res"])

    def norm_half(self, h512, gi, dst_fn, r_dst):
        S = self.S
        g0 = h512 * 512
        bank, rb = self.ps[4], self.r_ps[4]
        for c in range(KC):
            hb, rh = self.hring.next()
            S.dma("sp", lambda q, hb=hb, c=c: q.dma_start(out=hb[:], in_=self.HT[c, :, g0:g0 + 512]),
                  rh, reads=[self.r_HT[h512][c]], writes=[rh])
            sq, rsq = self.sqring.next()
            S.op("act", lambda a, sq=sq, hb=hb: a.activation(out=sq[:], in_=hb[:], func=AF.Square),
                 reads=[rh], writes=[rsq])
            S.op("pe", lambda t, sq=sq, c=c: t.matmul(bank[:], lhsT=self.ones_b[:], rhs=sq[:],
                                                      start=(c == 0), stop=(c == KC - 1)),
                 reads=[rsq, self.r_const], writes=[rb], signal=True)
        S.op("act", lambda a: a.activation(out=self.lnt[:], in_=bank[:], func=AF.Ln,
                                           scale=1.0 / D, bias=self.eps_ap()),
             reads=[rb, self.r_const], writes=[self.r_lnt])
        S.op("act", lambda a: a.activation(out=self.rstd[:], in_=self.lnt[:], func=AF.Exp, scale=-0.5),
             reads=[self.r_lnt], writes=[self.r_rstd])
        for c in range(KC):
            hb, rh = self.hring.next()
            S.dma("sp", lambda q, hb=hb, c=c: q.dma_start(out=hb[:], in_=self.HT[c, :, g0:g0 + 512]),
                  rh, reads=[self.r_HT[h512][c]], writes=[rh])
            dst = dst_fn(c)
            S.op("dve", lambda v, hb=hb, c=c, dst=dst: v.scalar_tensor_tensor(
                out=dst, in0=hb[:], scalar=self.gT[:, gi, c:c + 1],
                in1=self.rstd[:], op0=ALU.mult, op1=ALU.mult),
                reads=[rh, self.r_rstd, self.r_const],
                writes=([r_dst] if c == 0 else ()), swrites=(() if c == 0 else [r_dst]))

    def norm_tile(self, tile, gi):
        for tt in range(TA // 512):
            self.norm_half(tile * 2 + tt, gi,
                           lambda c, tt=tt: self.actT[:, c, tt * 512:(tt + 1) * 512], self.r_actT[tt])

    def eps_ap(self):
        return self.epsT[:, 0:1]

    def load_w(self, wap, col0, ncols, slot=None, off=0):
        S = self.S
        if slot is None:
            slot = self.wring.next()
        wt, rw = slot
        src = wap[:, col0:col0 + ncols].rearrange("(c p) n -> p c n", p=128)
        S.dma("pool", lambda q, wt=wt, src=src: q.dma_start(out=wt[:, :, off:off + ncols], in_=src), rw,
              writes=([rw] if off == 0 else ()), swrites=(() if off == 0 else [rw]))
        return slot

    def mm_fm(self, wt, rw, wcol, tt, nk=KC, act=None, ract=None):
        bank, rb = self.gemm_bank()
        S = self.S
        act = act if act is not None else self.actT
        ract = ract if ract is not None else self.r_actT[tt]
        for c in range(nk):
            S.op("pe", lambda t, c=c, bank=bank, wt=wt, wcol=wcol, tt=tt, act=act: t.matmul(
                bank[:], lhsT=wt[:, c, wcol:wcol + 128], rhs=act[:, c, tt * 512:(tt + 1) * 512],
                start=(c == 0), stop=(c == nk - 1)),
                reads=[rw, ract], writes=[rb], signal=(c == nk - 1))
        return bank, rb

    def mm_tm(self, wt, rw, tb):
        bank, rb = self.gemm_bank()
        S = self.S
        tt = tb // 4
        for c in range(KC):
            S.op("pe", lambda t, c=c, bank=bank, wt=wt, tb=tb: t.matmul(
                bank[:], lhsT=self.actT[:, c, tb * 128:(tb + 1) * 128], rhs=wt[:, c, :],
                start=(c == 0), stop=(c == KC - 1)),
                reads=[rw, self.r_actT[tt]], writes=[rb], signal=(c == KC - 1))
        return bank, rb

    def phase_a(self, layer, si, tile):
        S = self.S
        even = (layer % 2 == 0)
        if self.normed != ("a", layer, tile):
            self.norm_tile(tile, layer)
        self.normed = None
        t0 = tile * TA - self.tokbase[si]
        wap = self.w["ev_w_in"] if even else self.w["od_w_in"]
        ncb = 12 if even else 14
        wslots = {0: self.load_w(wap, 0, 512)}
        for jb in range(ncb):
            col0 = jb * 512
            if jb + 1 < ncb:
                wslots[jb + 1] = self.load_w(wap, (jb + 1) * 512, 512)
            wt, rw = wslots.pop(jb)
            kind, blk0, vcol0 = self._col_kind(even, col0)
            if kind == "fm":
                for jj in range(4):
                    st, rs = self.stg_b.next()
                    for tt in range(TA // 512):
                        bank, rb = self.mm_fm(wt, rw, jj * 128, tt)
                        self.copy(self.ev_eng(), st[:, tt * 512:(tt + 1) * 512], bank[:], reads=[rb],
                                  writes=([rs] if tt == 0 else ()), swrites=(() if tt == 0 else [rs]))
                    dst = self.QKT[si][blk0 + jj, :, t0:t0 + TA]
                    S.dma("sp", lambda q, st=st, dst=dst: q.dma_start(out=dst, in_=st[:]), rs, reads=[rs],
                          swrites=[self.r_QKT[si]])
            elif kind == "tm":
                for tb in range(TA // 128):
                    bank, rb = self.mm_tm(wt, rw, tb)
                    st, rs = self.stg_b.next()
                    self.copy(self.ev_eng(), st[:, 0:512], bank[:], reads=[rb], writes=[rs])
                    dst = self.VTM[si][t0 + tb * 128:t0 + (tb + 1) * 128, vcol0:vcol0 + 512]
                    S.dma("sp", lambda q, st=st, dst=dst: q.dma_start(out=dst, in_=st[:, 0:512]), rs, reads=[rs],
                          swrites=[self.r_VTM[si]])
            else:
                self._rope_block(si, t0, wt, rw, blk0, 0 if col0 < 6144 else 1)
        self.rope_flush()

    def _col_kind(self, even, col0):
        if even:
            if col0 < 2048:
                return "fm", col0 // 128, None
            if col0 < 3072:
                return "tm", None, col0 - 2048
            if col0 < 5120:
                return "fm", 16 + (col0 - 3072) // 128, None
            return "tm", None, 1024 + (col0 - 5120)
        else:
            if col0 < 3072:
                return "fm", col0 // 128, None
            if col0 < 4608:
                return "tm", None, col0 - 3072
            if col0 < 6656:
                return "rope", 24 + (col0 - 4608) // 128, None
            return "tm", None, 1536 + (col0 - 6656)

    def _mk_rope(self, L, A):
        S, RL = self.S, L["_res"]
        L["stq"] = Ring(S, A, "stq", 2, [128, 4, 128], BF16, dma=True, reslist=RL)
        L["cs"] = Ring(S, A, "cs", 3, [128, 2, 64], F32, dma=True, reslist=RL)
        L["yb"] = Ring(S, A, "yb", 4, [128, 512], BF16, reslist=RL)
        for n, sh in (("sqf", [128, 512]), ("xn", [128, 512]), ("ta", [128, 256]), ("tb", [128, 256]),
                      ("tc", [128, 256]), ("td", [128, 256]),
                      ("ss", [128, 4]), ("ln", [128, 4]), ("rs", [128, 4])):
            L[n] = Ring(S, A, "rp_" + n, 2, sh, F32, reslist=RL)
        L["g"] = A.alloc("ropeg", [128, 2, 128], F32)
        L["r_g"] = S.res("ropeg", dma=True)
        RL.append(L["r_g"])
        L["g_loaded"] = False

    def _rope_block(self, si, t0, wt, rw, blk0, gidx):
        S = self.S
        R = self.use_layout("rope", self._mk_rope)
        if R["g_loaded"] != self.layout_epoch:
            R["g_loaded"] = self.layout_epoch
            src = self.w["od_qk_norm_g"].rearrange("(o a) d -> o a d", o=1).broadcast_to([128, 2, 128])
            S.dma("sp", lambda q: q.dma_start(out=R["g"][:], in_=src), R["r_g"], writes=[R["r_g"]])
        for tb in range(TA // 128):
            tok = t0 + tb * 128
            bank, rb = self.mm_tm(wt, rw, tb)
            cs, rcs = R["cs"].next()
            S.dma("sp", lambda q, cs=cs, tok=tok: q.dma_start(out=cs[:, 0, :], in_=self.w["c_cos"][tok:tok + 128, :]),
                  rcs, writes=[rcs])
            S.dma("sp", lambda q, cs=cs, tok=tok: q.dma_start(out=cs[:, 1, :], in_=self.w["c_sin"][tok:tok + 128, :]),
                  rcs, swrites=[rcs])
            sqf, r1 = R["sqf"].next()
            S.op("act", lambda a, bank=bank, sqf=sqf: a.activation(out=sqf[:], in_=bank[:], func=AF.Square),
                 reads=[rb], writes=[r1])
            ss, r2 = R["ss"].next()
            S.op("dve", lambda v, ss=ss, sqf=sqf: v.tensor_reduce(out=ss[:], in_=sqf[:].rearrange("p (h d) -> p h d", h=4),
                                                                  axis=AX.X, op=ALU.add), reads=[r1], writes=[r2])
            ln, rln = R["ln"].next()
            S.op("act", lambda a, ln=ln, ss=ss: a.activation(out=ln[:], in_=ss[:], func=AF.Ln, scale=1.0 / HD,
                                                             bias=self.eps_ap()), reads=[r2, self.r_const], writes=[rln])
            rs, rrs = R["rs"].next()
            S.op("act", lambda a, rs=rs, ln=ln: a.activation(out=rs[:], in_=ln[:], func=AF.Exp, scale=-0.5),
                 reads=[rln], writes=[rrs])
            xn, rxn = R["xn"].next()
            for hh in range(4):
                S.op("dve", lambda v, bank=bank, xn=xn, rs=rs, hh=hh: v.scalar_tensor_tensor(
                    out=xn[:, hh * 128:(hh + 1) * 128], in0=bank[:, hh * 128:(hh + 1) * 128], scalar=rs[:, hh:hh + 1],
                    in1=R["g"][:, gidx, :], op0=ALU.mult, op1=ALU.mult),
                    reads=[rb, rrs, R["r_g"]], writes=([rxn] if hh == 0 else ()), swrites=(() if hh == 0 else [rxn]))
            pr = lambda t: t[:].rearrange("p (h i two) -> p h i two", h=4, two=2)
            x0, x1 = pr(xn)[:, :, :, 0], pr(xn)[:, :, :, 1]
            cosb = cs[:, 0, :].unsqueeze(1).broadcast_to([128, 4, 64])
            sinb = cs[:, 1, :].unsqueeze(1).broadcast_to([128, 4, 64])
            (ta, r_ta), (tb_, r_tb), (tc, r_tc), (td, r_td) = R["ta"].next(), R["tb"].next(), R["tc"].next(), R["td"].next()
            yb, ryb = R["yb"].next()
            y0, y1 = pr(yb)[:, :, :, 0], pr(yb)[:, :, :, 1]
            v3 = lambda t: t[:].rearrange("p (h i) -> p h i", h=4)
            S.op("dve", lambda v, x0=x0, cosb=cosb, ta=ta: v.tensor_tensor(out=v3(ta), in0=x0, in1=cosb, op=ALU.mult),
                 reads=[rxn, rcs], writes=[r_ta])
            S.op("dve", lambda v, x1=x1, sinb=sinb, tb_=tb_: v.tensor_tensor(out=v3(tb_), in0=x1, in1=sinb, op=ALU.mult),
                 reads=[rxn, rcs], writes=[r_tb])
            S.op("dve", lambda v, y0=y0, ta=ta, tb_=tb_: v.tensor_tensor(out=y0, in0=v3(ta), in1=v3(tb_), op=ALU.subtract),
                 reads=[r_ta, r_tb], writes=[ryb])
            S.op("pool", lambda v, x0=x0, sinb=sinb, tc=tc: v.tensor_tensor(out=v3(tc), in0=x0, in1=sinb, op=ALU.mult),
                 reads=[rxn, rcs], writes=[r_tc])
            S.op("pool", lambda v, x1=x1, cosb=cosb, td=td: v.tensor_tensor(out=v3(td), in0=x1, in1=cosb, op=ALU.mult),
                 reads=[rxn, rcs], writes=[r_td])
            S.op("pool", lambda v, y1=y1, tc=tc, td=td: v.tensor_tensor(out=y1, in0=v3(tc), in1=v3(td), op=ALU.add),
                 reads=[r_tc, r_td], swrites=[ryb])
            def back(tb=tb, yb=yb, ryb=ryb, tok=tok):
                pb, rpb_ = self.psbs[tb % 2], self.r_psb[tb % 2]
                for hh in range(4):
                    S.op("pe", lambda t, hh=hh: t.matmul(
                        pb[:, hh * 128:(hh + 1) * 128], lhsT=yb[:, hh * 128:(hh + 1) * 128], rhs=self.ident_b[:],
                        start=True, stop=True),
                        reads=[ryb, self.r_const2], writes=[rpb_], signal=(hh == 3))
                stq, rsq_ = R["stq"].next()
                self.copy(self.ev_eng(), stq[:], pb[:, 0:512].rearrange("p (h t) -> p h t", h=4),
                          reads=[rpb_], writes=[rsq_])
                dst = self.QKT[si][blk0:blk0 + 4, :, tok:tok + 128].rearrange("h p t -> p h t")
                S.dma("sp", lambda q: q.dma_start(out=dst, in_=stq[:]), rsq_, reads=[rsq_],
                      swrites=[self.r_QKT[si]])
            self.rope_pending.append(back)
            while len(self.rope_pending) > 2:
                self.rope_pending.pop(0)()

    def rope_flush(self):
        while self.rope_pending:
            self.rope_pending.pop(0)()

    def resid_store(self, bank, rb, ob, h512):
        S = self.S
        g0 = h512 * 512
        hb, rh = self.hring.next()
        S.dma("sp", lambda q, hb=hb: q.dma_start(out=hb[:], in_=self.HT[ob, :, g0:g0 + 512]),
              rh, reads=[self.r_HT[h512][ob]], writes=[rh])
        st, rs = self.stg_f.next()
        S.op("dve", lambda v, st=st, hb=hb, bank=bank: v.tensor_tensor(out=st[:], in0=bank[:], in1=hb[:], op=ALU.add),
             reads=[rb, rh], writes=[rs])
        S.dma("sp", lambda q, st=st: q.dma_start(out=self.HT[ob, :, g0:g0 + 512], in_=st[:]),
              rs, reads=[rs], swrites=[self.r_HT[h512][ob]])

    def phase_c(self, layer, si, tile):
        S = self.S
        t0 = tile * TA - self.tokbase[si]
        for tb in range(TA // 128):
            ao, rao = self.aoin.next()
            S.dma("sp", lambda q, ao=ao, tb=tb: q.dma_start(
                out=ao[:], in_=self.AO[si][t0 + tb * 128:t0 + (tb + 1) * 128, :]),
                rao, reads=[self.r_AO[si]], writes=[rao])
            for cg in range(4):
                half = cg % 2
                po = 0
                pb = self.psbs[half]
                for j in range(4):
                    c = cg * 4 + j
                    S.op("pe", lambda t, ao=ao, c=c, j=j, po=po, pb=pb: t.matmul(
                        pb[:, po + j * 128:po + (j + 1) * 128], lhsT=ao[:, c * 128:(c + 1) * 128], rhs=self.ident_b[:],
                        start=True, stop=True),
                        reads=[rao, self.r_const2], writes=[self.r_psb[half]], signal=(j == 3))
                first = (tb % 4 == 0 and cg == 0)
                ra = self.r_actT[tb // 4]
                self.copy(self.ev_eng(), self.actT[:, cg * 4:(cg + 1) * 4, tb * 128:(tb + 1) * 128],
                          pb[:, po:po + 512].rearrange("p (c t) -> p c t", c=4),
                          reads=[self.r_psb[half]], writes=([ra] if first else ()), swrites=(() if first else [ra]))
        wap = self.w["ev_w_out"] if layer % 2 == 0 else self.w["od_w_out"]
        for jb in range(4):
            wt, rw = self.load_w(wap, jb * 512, 512)
            for jj in range(4):
                ob = jb * 4 + jj
                for tt in range(TA // 512):
                    bank, rb = self.mm_fm(wt, rw, jj * 128, tt)
                    self.resid_store(bank, rb, ob, tile * 2 + tt)

    def _mk_ffn(self, L, A):
        S = self.S
        L["aT"] = A.alloc("aT", [128, FC, TA], BF16)
        L["r_aT"] = [S.res("aT0"), S.res("aT1")]
        L["_res"] += L["r_aT"]
        L["sg"] = Ring(S, A, "sg", 2, [128, 512], F32, reslist=L["_res"])

    def phase_d(self, layer, tile, nxt=None):
        S = self.S
        if self.normed != ("d", layer, tile):
            self.norm_tile(tile, 2 + layer)
        self.normed = None
        L = self.use_layout("ffn", self._mk_ffn)
        aT, r_aT = L["aT"], L["r_aT"]
        wg, wu, wd = self.w["ffn_w_gate"][layer], self.w["ffn_w_up"][layer], self.w["ffn_w_down"][layer]
        for fp in range(FC // 2):
            slot = self.wring.next()
            self.load_w(wg, fp * 256, 256, slot, off=0)
            self.load_w(wu, fp * 256, 256, slot, off=256)
            wt, rw = slot
            for j in range(2):
                fb = fp * 2 + j
                for tt in range(TA // 512):
                    bg, rbg = self.mm_fm(wt, rw, j * 128, tt)
                    bu, rbu = self.mm_fm(wt, rw, 256 + j * 128, tt)
                    sg, rsg = L["sg"].next()
                    S.op("act", lambda a, sg=sg, bg=bg: a.activation(out=sg[:], in_=bg[:], func=AF.Silu),
                         reads=[rbg], writes=[rsg])
                    first = (fb == 0)
                    S.op("dve", lambda v, sg=sg, bu=bu, fb=fb, tt=tt: v.tensor_tensor(
                        out=aT[:, fb, tt * 512:(tt + 1) * 512], in0=sg[:], in1=bu[:], op=ALU.mult),
                        reads=[rsg, rbu], writes=([r_aT[tt]] if first else ()), swrites=(() if first else [r_aT[tt]]))
        for ob in range(KC):
            if nxt is not None and ob in (2, 9):
                tt_ = 0 if ob == 2 else 1
                self.norm_half(nxt[2] * 2 + tt_, nxt[1] if nxt[0] == "a" else 2 + nxt[1],
                               lambda c, tt_=tt_: self.actT[:, c, tt_ * 512:(tt_ + 1) * 512], self.r_actT[tt_])
                self.normed = nxt
            slot = self.wring.next()
            idx = (self.wring.i - 1) % len(self.wring.slots)
            wv = self.wd_views[idx]
            _, rw = slot
            src = wd[:, ob * 128:(ob + 1) * 128].rearrange("(c p) n -> p c n", p=128)
            S.dma("pool", lambda q, wv=wv, src=src: q.dma_start(out=wv[:], in_=src), rw, writes=[rw])
            for tt in range(TA // 512):
                bank, rb = self.gemm_bank()
                for fc in range(FC):
                    S.op("pe", lambda t, fc=fc, bank=bank, wv=wv, tt=tt: t.matmul(
                        bank[:], lhsT=wv[:, fc, :], rhs=aT[:, fc, tt * 512:(tt + 1) * 512],
                        start=(fc == 0), stop=(fc == FC - 1)),
                        reads=[rw, r_aT[tt]], writes=[rb], signal=(fc == FC - 1))
                self.resid_store(bank, rb, ob, tile * 2 + tt)

    def phase_out(self, si):
        S = self.S
        L = self.use_layout("io", self._mk_io)
        Sq = self.seqs[si]
        nh = Sq // 512
        base = self.tokbase[si] // 512

        def do_norm(ht):
            ynT, r_ynT = L["ynT"].next()
            self.norm_half(base + ht, 4, lambda c, ynT=ynT: ynT[:, c, :], r_ynT)
            return ynT, r_ynT

        cur = do_norm(0)
        for ht in range(nh):
            nxt = do_norm(ht + 1) if ht + 1 < nh else None
            ynT, r_ynT = cur
            for tb in range(4):
                xt, rx = L["xin"].next()
                for cg in range(4):
                    bank, rb = self.ps[5 + cg % 2], self.r_ps[5 + cg % 2]
                    for j in range(4):
                        c = cg * 4 + j
                        S.op("pe", lambda t, bank=bank, j=j, c=c, tb=tb, ynT=ynT: t.transpose(
                            bank[:, j * 128:(j + 1) * 128], ynT[:, c, tb * 128:(tb + 1) * 128], self.ident_f[:]),
                            reads=[r_ynT, self.r_const], writes=[rb], signal=(j == 3))
                    first = (cg == 0)
                    self.copy(self.ev_eng(), xt[:, cg * 512:(cg + 1) * 512], bank[:], reads=[rb],
                              writes=([rx] if first else ()), swrites=(() if first else [rx]))
                r0 = ht * 512 + tb * 128
                S.dma("sp", lambda q, xt=xt, r0=r0: q.dma_start(out=self.y[si][r0:r0 + 128, :], in_=xt[:]),
                      rx, reads=[rx], swrites=[self.r_y])
            cur = nxt

    def _mk_dense(self, L, A):
        S, RL = self.S, L["_res"]
        L["k"] = Ring(S, A, "dk", 4, [128, 4096], BF16, dma=True, reslist=RL)
        L["q"] = Ring(S, A, "dq_", 4, [128, 512], BF16, dma=True, reslist=RL)
        L["v"] = Ring(S, A, "dv", 2, [128, 32, 257], BF16, dma=True, reslist=RL)
        L["pt"] = Ring(S, A, "dpt", 4, [128, 512], BF16, reslist=RL)
        L["x"] = Ring(S, A, "dx", 2, [128, 512], F32, reslist=RL)
        L["ostg"] = Ring(S, A, "dost", 2, [128, 4, 256], BF16, dma=True, reslist=RL)
        L["dq"] = A.alloc("dqc", [128, 5, 512], F32)
        L["ab"] = A.alloc("abias", [128, 4, 32], F32)
        L["lamT"] = A.alloc("lamT", [128, 4, 128], F32)
        L["gsub"] = A.alloc("gsub", [128, 256], F32)
        L["r_c"] = S.res("densec", dma=True)
        RL.append(L["r_c"])
        for n, sh in (("o1", [128, 4, 256]), ("of", [128, 256]), ("junk", [128, 256]), ("rden", [128, 4]),
                      ("nlr", [128, 4]), ("ssq", [128, 1]), ("lnq", [128, 1]), ("rsq", [128, 1]),
                      ("lp", [128, 2, 128]), ("ls", [128, 2]), ("le", [128, 2]), ("nlam", [128, 1])):
            L[n] = A.alloc("d_" + n, sh, F32)
            L["r_" + n] = S.res("d_" + n)
            RL.append(L["r_" + n])

    def dense_attn(self, si, mode, layer):
        S = self.S
        Sq = self.seqs[si]
        nkb, nqt = Sq // 128, Sq // 512
        L = self.use_layout("dense_" + mode, self._mk_dense)
        dv = 257 if mode == "diff" else 129
        rc = L["r_c"]
        for (vt, rv) in L["v"].slots:
            S.op("dve", lambda v, vt=vt: v.memset(vt[:, :, dv - 1:dv], 1.0), writes=[rv])
        if mode == "diff":
            S.dma("sp", lambda q: q.dma_start(out=L["dq"][:].rearrange("p a b -> p (a b)"), in_=self.w["c_dq"]),
                  rc, writes=[rc])
            S.dma("sp", lambda q: q.dma_start(out=L["ab"][:].rearrange("p a b -> p (a b)"), in_=self.w["c_ab"]),
                  rc, swrites=[rc])
            S.dma("sp", lambda q: q.dma_start(
                out=L["lamT"][:], in_=self.w["ev_lambda"].rearrange("(o a) d -> o a d", o=1).broadcast_to([128, 4, 128])),
                rc, swrites=[rc])
            S.dma("sp", lambda q: q.dma_start(out=L["gsub"][:], in_=self.w["ev_subln_g"].broadcast_to([128, 256])),
                  rc, swrites=[rc])
            lam_init = 0.8 - 0.6 * math.exp(-0.3 * layer)
            lamT = L["lamT"]
            S.op("dve", lambda v: v.tensor_tensor(out=L["lp"][:], in0=lamT[:, 0:4:2, :], in1=lamT[:, 1:4:2, :],
                                                  op=ALU.mult), reads=[rc], writes=[L["r_lp"]])
            S.op("dve", lambda v: v.tensor_reduce(out=L["ls"][:], in_=L["lp"][:], axis=AX.X, op=ALU.add),
                 reads=[L["r_lp"]], writes=[L["r_ls"]])
            S.op("act", lambda a: a.activation(out=L["le"][:], in_=L["ls"][:], func=AF.Exp),
                 reads=[L["r_ls"]], writes=[L["r_le"]])
            S.op("dve", lambda v: v.scalar_tensor_tensor(out=L["nlam"][:], in0=L["le"][:, 1:2], scalar=-lam_init,
                                                         in1=L["le"][:, 0:1], op0=ALU.add, op1=ALU.subtract),
                 reads=[L["r_le"]], writes=[L["r_nlam"]])
            S.op("dve", lambda v: v.tensor_scalar(out=L["gsub"][:], in0=L["gsub"][:], scalar1=1.0 - lam_init,
                                                  scalar2=None, op0=ALU.mult), reads=[rc], writes=[rc])
            slopes = _alibi(4)
        accs = [2, 3, 4, 5, 6, 7]
        acc_i = [0]

        def next_acc():
            i = accs[acc_i[0] % len(accs)]
            acc_i[0] += 1
            return self.ps[i], self.r_ps[i]

        sc_i = [0]
        steps = []
        prel = {}
        steps_done = []
        AOs = self.AO[si]

        def add_tile(pre, kt, rk, qsrc, vt, rv, qt, h, fin):
            st = {"banks": None, "q": None}
            if pre is not None:
                prel.setdefault(max(0, len(steps) - 24), []).append(pre)

            def mk(kb):
                hold = {}

                def front():
                    for p_ in prel.pop(len(steps_done), []):
                        p_()
                    steps_done.append(1)
                    if kb == 0:
                        qtile, rq = L["q"].next()
                        S.dma("sp", lambda q, qtile=qtile: q.dma_start(out=qtile[:], in_=qsrc),
                              rq, reads=[self.r_QKT[si]], writes=[rq])
                        st["q"] = (qtile, rq)
                    qtile, rq = st["q"]
                    sc, rsc = self.ps[sc_i[0] % 2], self.r_ps[sc_i[0] % 2]
                    sc_i[0] += 1
                    S.op("pe", lambda t, sc=sc, qtile=qtile: t.matmul(
                        sc[:], lhsT=kt[:, kb * 128:(kb + 1) * 128], rhs=qtile[:], start=True, stop=True),
                        reads=[rk, rq], writes=[rsc])
                    pt, rpt = L["pt"].next()
                    if mode == "diff":
                        d0 = 512 * qt - 128 * kb
                        sl = slopes[h]
                        if d0 >= 128:
                            dsel, coef, bcol = 0, -sl / SCALE, d0 // 128
                        elif d0 <= -512:
                            dsel, coef, bcol = 0, sl / SCALE, (-d0) // 128
                        else:
                            dsel, coef, bcol = 1 + (-d0) // 128, -sl / SCALE, 0
                        xt, rx = L["x"].next()
                        S.op("dve", lambda v, xt=xt, sc=sc: v.scalar_tensor_tensor(
                            out=xt[:], in0=L["dq"][:, dsel, :], scalar=coef, in1=sc[:], op0=ALU.mult, op1=ALU.add),
                            reads=[rsc, rc], writes=[rx])
                        S.op("act", lambda a, pt=pt, xt=xt: a.activation(
                            out=pt[:], in_=xt[:], func=AF.Exp, scale=SCALE, bias=L["ab"][:, h, bcol:bcol + 1]),
                            reads=[rx, rc], writes=[rpt])
                    else:
                        S.op("act", lambda a, pt=pt, sc=sc: a.activation(out=pt[:], in_=sc[:], func=AF.Exp, scale=SCALE),
                             reads=[rsc], writes=[rpt])
                    hold["pt"] = (pt, rpt)

                def back():
                    if kb == 0:
                        st["banks"] = [next_acc() for _ in range(4)]
                    pt, rpt = hold["pt"]
                    for qs in range(4):
                        bk, rbk = st["banks"][qs]
                        S.op("pe", lambda t, bk=bk, pt=pt, qs=qs: t.matmul(
                            bk[:, 0:dv], lhsT=pt[:, qs * 128:(qs + 1) * 128], rhs=vt[:, kb, 0:dv],
                            start=(kb == 0), stop=(kb == nkb - 1)),
                            reads=[rpt, rv], writes=[rbk], signal=(qs == 3))
                    if kb == nkb - 1:
                        fin(st["banks"])
                return front, back

            for kb in range(nkb):
                steps.append(mk(kb))

        def run_pipeline(lag):
            n = len(steps)
            for i in range(n + lag):
                if i < n:
                    steps[i][0]()
                if i - lag >= 0:
                    steps[i - lag][1]()

        if mode == "gqa":
            for kvh in range(4):
                cur = {}

                def pre_kv(kvh=kvh, cur=cur):
                    kt, rk = cur["k"]
                    vt, rv = cur["v"]
                    S.dma("sp", lambda q: q.dma_start(out=kt[:, 0:Sq], in_=self.QKT[si][36 + kvh]),
                          rk, reads=[self.r_QKT[si]], writes=[rk])
                    vsrc = self.VTM[si][:, 1536 + kvh * 128:1536 + (kvh + 1) * 128].rearrange("(n p) d -> p n d", p=128)
                    S.dma("sp", lambda q: q.dma_start(out=vt[:, 0:nkb, 0:128], in_=vsrc),
                          rv, reads=[self.r_VTM[si]], swrites=[rv])

                cur["k"] = L["k"].next()
                cur["v"] = L["v"].next()
                kt, rk = cur["k"]
                vt, rv = cur["v"]
                for g in range(3):
                    h = kvh * 3 + g
                    for qt in range(nqt):
                        def fin(banks, h=h, qt=qt):
                            ost, ro = L["ostg"].next()
                            for qs in range(4):
                                bk, rbk = banks[qs]
                                S.op("dve", lambda v, bk=bk, qs=qs: v.reciprocal(out=L["rden"][:, qs:qs + 1], in_=bk[:, 128:129]),
                                     reads=[rbk], writes=[L["r_rden"]])
                                S.op("dve", lambda v, bk=bk, qs=qs, ost=ost: v.tensor_scalar(
                                    out=ost[:, qs, 0:128], in0=bk[:, 0:128], scalar1=L["rden"][:, qs:qs + 1], scalar2=None,
                                    op0=ALU.mult), reads=[rbk, L["r_rden"]],
                                    writes=([ro] if qs == 0 else ()), swrites=(() if qs == 0 else [ro]))
                            dst = AOs[qt * 512:(qt + 1) * 512, 512 + h * 128:512 + (h + 1) * 128].rearrange(
                                "(s p) d -> p s d", p=128)
                            S.dma("sp", lambda q, ost=ost, dst=dst: q.dma_start(out=dst, in_=ost[:, :, 0:128]),
                                  ro, reads=[ro], swrites=[self.r_AO[si]])
                        qsrc = self.QKT[si][24 + h, :, qt * 512:(qt + 1) * 512]
                        add_tile(pre_kv if (g == 0 and qt == 0) else None, kt, rk, qsrc, vt, rv, qt, h, fin)
            run_pipeline(2)
            return
        for h in range(4):
            vt, rv = L["v"].next()
            ks = [L["k"].next() for _ in range(2)]

            def pre_h(h=h, vt=vt, rv=rv, ks=ks):
                vsrc = self.VTM[si][:, h * 256:(h + 1) * 256].rearrange("(n p) d -> p n d", p=128)
                S.dma("sp", lambda q: q.dma_start(out=vt[:, 0:nkb, 0:256], in_=vsrc),
                      rv, reads=[self.r_VTM[si]], swrites=[rv])
                for c in range(2):
                    kt, rk = ks[c]
                    S.dma("sp", lambda q, kt=kt, c=c: q.dma_start(out=kt[:, 0:Sq], in_=self.QKT[si][8 + 2 * h + c]),
                          rk, reads=[self.r_QKT[si]], writes=[rk])

            for qt in range(nqt):
                hold = {}
                for c in range(2):
                    def fin(banks, h=h, qt=qt, c=c, hold=hold):
                        if c == 0:
                            hold["ost"] = L["ostg"].next()
                        ost, ro = hold["ost"]
                        for qs in range(4):
                            bk, rbk = banks[qs]
                            S.op("dve", lambda v, bk=bk, qs=qs: v.reciprocal(out=L["rden"][:, qs:qs + 1], in_=bk[:, 256:257]),
                                 reads=[rbk], writes=[L["r_rden"]])
                            if c == 0:
                                S.op("act", lambda a, bk=bk, qs=qs: a.activation(
                                    out=L["o1"][:, qs, :], in_=bk[:, 0:256], func=AF.Copy, scale=L["rden"][:, qs:qs + 1]),
                                    reads=[rbk, L["r_rden"]], writes=[L["r_o1"]] if qs == 0 else (),
                                    swrites=() if qs == 0 else [L["r_o1"]])
                            else:
                                S.op("dve", lambda v, qs=qs: v.tensor_tensor(
                                    out=L["nlr"][:, qs:qs + 1], in0=L["rden"][:, qs:qs + 1], in1=L["nlam"][:], op=ALU.mult),
                                    reads=[L["r_rden"], L["r_nlam"]], writes=[L["r_nlr"]])
                                S.op("dve", lambda v, bk=bk, qs=qs: v.scalar_tensor_tensor(
                                    out=L["of"][:], in0=bk[:, 0:256], scalar=L["nlr"][:, qs:qs + 1], in1=L["o1"][:, qs, :],
                                    op0=ALU.mult, op1=ALU.add), reads=[rbk, L["r_nlr"], L["r_o1"]], writes=[L["r_of"]])
                                S.op("act", lambda a: a.activation(out=L["junk"][:], in_=L["of"][:], func=AF.Square,
                                                                   accum_out=L["ssq"][:]),
                                     reads=[L["r_of"]], writes=[L["r_junk"], L["r_ssq"]])
                                S.op("act", lambda a: a.activation(out=L["lnq"][:], in_=L["ssq"][:], func=AF.Ln,
                                                                   scale=1.0 / 256, bias=self.eps_ap()),
                                     reads=[L["r_ssq"], self.r_const], writes=[L["r_lnq"]])
                                S.op("act", lambda a: a.activation(out=L["rsq"][:], in_=L["lnq"][:], func=AF.Exp, scale=-0.5),
                                     reads=[L["r_lnq"]], writes=[L["r_rsq"]])
                                S.op("dve", lambda v, qs=qs, ost=ost: v.scalar_tensor_tensor(
                                    out=ost[:, qs, :], in0=L["of"][:], scalar=L["rsq"][:, 0:1], in1=L["gsub"][:],
                                    op0=ALU.mult, op1=ALU.mult), reads=[L["r_of"], L["r_rsq"], rc],
                                    writes=([ro] if qs == 0 else ()), swrites=(() if qs == 0 else [ro]))
                        if c == 1:
                            dst = AOs[qt * 512:(qt + 1) * 512, h * 256:(h + 1) * 256].rearrange("(s p) d -> p s d", p=128)
                            S.dma("sp", lambda q, ost=ost, dst=dst: q.dma_start(out=dst, in_=ost[:]),
                                  ro, reads=[ro], swrites=[self.r_AO[si]])
                    qsrc = self.QKT[si][2 * h + c, :, qt * 512:(qt + 1) * 512]
                    add_tile(pre_h if (qt == 0 and c == 0) else None, ks[c][0], ks[c][1], qsrc, vt, rv, qt, h, fin)
        run_pipeline(2)

    def _mk_na(self, L, A):
        S, RL = self.S, L["_res"]
        L["q"] = Ring(S, A, "nq", 2, [128, 4096], BF16, dma=True, reslist=RL)
        L["k"] = Ring(S, A, "nk", 2, [128, 4096], BF16, dma=True, reslist=RL)
        L["ve"] = Ring(S, A, "nve", 2, [128, 32, 129], BF16, dma=True, reslist=RL)
        L["vo"] = Ring(S, A, "nvo", 2, [128, 32, 129], BF16, dma=True, reslist=RL)
        L["bh"] = Ring(S, A, "nbh", 2, [128, 14, 64], F32, dma=True, reslist=RL)
        L["x"] = Ring(S, A, "nx", 3, [128, 4, 64], F32, reslist=RL)
        L["pt"] = Ring(S, A, "npt", 4, [128, 4, 64], BF16, reslist=RL)
        L["ostg"] = Ring(S, A, "nost", 2, [64, 8, 128], BF16, dma=True, reslist=RL)
        L["cm"] = A.alloc("ncm", [128, 64], F32)
        L["r_c"] = S.res("nac", dma=True)
        L["rden"] = A.alloc("nrden", [64, 1], F32)
        L["r_rden"] = S.res("nrden")
        RL += [L["r_c"], L["r_rden"]]

    def na_attn(self, si):
        S = self.S
        Sq = self.seqs[si]
        R = Sq // 64
        nb = Sq // 128
        L = self.use_layout("na", self._mk_na)
        rc = L["r_c"]
        S.dma("sp", lambda q: q.dma_start(out=L["cm"][:], in_=self.w["c_cm"]), rc, writes=[rc])
        for ring in (L["ve"], L["vo"]):
            for (vt, rv) in ring.slots:
                S.op("dve", lambda v, vt=vt: v.memset(vt[:, :, 128:129], 1.0), writes=[rv])
        cnt = {"acc": 0, "sc": 0}
        units = []
        for h in range(8):
            hs = {}

            def loads(h=h, hs=hs):
                qt_, rq = L["q"].next()
                S.dma("sp", lambda q: q.dma_start(out=qt_[:, 0:Sq], in_=self.QKT[si][16 + h]),
                      rq, reads=[self.r_QKT[si]], writes=[rq])
                kt, rk = L["k"].next()
                S.dma("sp", lambda q: q.dma_start(out=kt[:, 0:Sq], in_=self.QKT[si][24 + h]),
                      rk, reads=[self.r_QKT[si]], writes=[rk])
                ve, rve = L["ve"].next()
                vo, rvo = L["vo"].next()
                c0 = 1024 + h * 128
                se = self.VTM[si][:, c0:c0 + 128].rearrange("(n p) d -> p n d", p=128)
                so = self.VTM[si][64:Sq - 64, c0:c0 + 128].rearrange("(n p) d -> p n d", p=128)
                S.dma("sp", lambda q: q.dma_start(out=ve[:, 0:nb, 0:128], in_=se),
                      rve, reads=[self.r_VTM[si]], swrites=[rve])
                S.dma("sp", lambda q: q.dma_start(out=vo[:, 0:nb - 1, 0:128], in_=so),
                      rvo, reads=[self.r_VTM[si]], swrites=[rvo])
                bh, rbh = L["bh"].next()
                S.dma("sp", lambda q: q.dma_start(out=bh[:].rearrange("p a b -> p (a b)"), in_=self.w["rpbx"][h]),
                      rbh, writes=[rbh])
                S.op("dve", lambda v: v.tensor_tensor(
                    out=bh[:], in0=bh[:], in1=L["cm"][:].unsqueeze(1).broadcast_to([128, 14, 64]), op=ALU.add),
                    reads=[rc, rbh], writes=[rbh])
                hs.update(q=(qt_, rq), k=(kt, rk), ve=(ve, rve), vo=(vo, rvo), bh=(bh, rbh), c0=c0, ost=None)

            pre_at = max(0, len(units) - 12)
            for qr in range(R):
                def mk(qr=qr, hs=hs, first=(qr == 0), loads=loads):
                    hold = {}
                    r0 = min(max(qr - 4, 0), R - 8)
                    rho = r0 - qr + 7

                    def front():
                        qt_, rq = hs["q"]
                        kt, rk = hs["k"]
                        bh, rbh = hs["bh"]
                        sc, rsc = self.ps[cnt["sc"] % 2], self.r_ps[cnt["sc"] % 2]
                        cnt["sc"] += 1
                        for i in range(4):
                            ks = (r0 + 2 * i) * 64
                            S.op("pe", lambda t, ks=ks, i=i: t.matmul(
                                sc[:, i * 64:(i + 1) * 64], lhsT=kt[:, ks:ks + 128], rhs=qt_[:, qr * 64:(qr + 1) * 64],
                                start=True, stop=True), reads=[rk, rq], writes=[rsc], signal=(i == 3))
                        xt, rx = L["x"].next()
                        S.op("dve", lambda v: v.scalar_tensor_tensor(
                            out=xt[:], in0=sc[:, 0:256].rearrange("p (a b) -> p a b", a=4), scalar=SCALE,
                            in1=bh[:, rho:rho + 7:2, :], op0=ALU.mult, op1=ALU.add), reads=[rsc, rbh], writes=[rx])
                        pt, rpt = L["pt"].next()
                        S.op("act", lambda a: a.activation(out=pt[:], in_=xt[:], func=AF.Exp),
                             reads=[rx], writes=[rpt])
                        hold["pt"] = (pt, rpt)

                    def back():
                        pt, rpt = hold["pt"]
                        ve, rve = hs["ve"]
                        vo, rvo = hs["vo"]
                        ai = 2 + cnt["acc"] % 6
                        cnt["acc"] += 1
                        bk, rbk = self.ps[ai], self.r_ps[ai]
                        for i in range(4):
                            row = r0 + 2 * i
                            vsrc, rvs = (ve, rve) if row % 2 == 0 else (vo, rvo)
                            vb = row // 2
                            S.op("pe", lambda t, i=i, vsrc=vsrc, vb=vb: t.matmul(
                                bk[0:64, 0:129], lhsT=pt[:, i, :], rhs=vsrc[:, vb, :], start=(i == 0), stop=(i == 3)),
                                reads=[rpt, rvs], writes=[rbk], signal=(i == 3))
                        if qr % 8 == 0:
                            hs["ost"] = L["ostg"].next()
                        ost, ro = hs["ost"]
                        S.op("dve", lambda v: v.reciprocal(out=L["rden"][:], in_=bk[0:64, 128:129]),
                             reads=[rbk], writes=[L["r_rden"]])
                        fst = (qr % 8 == 0)
                        S.op("act", lambda a: a.activation(
                            out=ost[:, qr % 8, :], in_=bk[0:64, 0:128], func=AF.Copy, scale=L["rden"][:, 0:1]),
                            reads=[rbk, L["r_rden"]], writes=([ro] if fst else ()), swrites=(() if fst else [ro]))
                        if qr % 8 == 7:
                            q0 = (qr - 7) * 64
                            c0 = hs["c0"]
                            dst = self.AO[si][q0:q0 + 512, c0:c0 + 128].rearrange("(r p) d -> p r d", p=64)
                            S.dma("sp", lambda q: q.dma_start(out=dst, in_=ost[:]),
                                  ro, reads=[ro], swrites=[self.r_AO[si]])
                    return front, back
                units.append(list(mk()) + [[]])
            units[pre_at][2].append(loads)
        self.run_units(units, 2)

    def run_units(self, units, lag):
        n = len(units)
        for i in range(n + lag):
            if i < n:
                for p_ in units[i][2]:
                    p_()
                units[i][0]()
            if i - lag >= 0:
                units[i - lag][1]()

    def _mk_dil(self, L, A):
        S, RL = self.S, L["_res"]
        L["q"] = Ring(S, A, "lq", 2, [128, 4096], BF16, dma=True, reslist=RL)
        L["kp"] = Ring(S, A, "lkp", 2, [128, 6144], BF16, dma=True, reslist=RL)
        L["vp"] = Ring(S, A, "lvp", 2, [128, 48, 129], BF16, dma=True, reslist=RL)
        L["bd"] = Ring(S, A, "lbd", 2, [128, 256], F32, dma=True, reslist=RL)
        L["x"] = Ring(S, A, "lx", 3, [128, 256], F32, reslist=RL)
        L["pt"] = Ring(S, A, "lpt", 4, [128, 256], BF16, reslist=RL)
        L["cst"] = Ring(S, A, "lcst", 3, [128, 4, 132], F32, dma=True, reslist=RL)
        L["cnin"] = Ring(S, A, "lcn", 2, [128, 3, 4, 132], F32, dma=True, reslist=RL)
        L["ocs"] = Ring(S, A, "locs", 2, [128, 4, 128], BF16, dma=True, reslist=RL)
        L["s1"] = A.alloc("ls1", [128, 4, 132], F32)
        L["r_s1"] = S.res("ls1")
        L["rden"] = A.alloc("lrden", [128, 4], F32)
        L["r_rden"] = S.res("lrden")
        RL += [L["r_s1"], L["r_rden"]]

    def dil_attn(self, si):
        S = self.S
        Sq = self.seqs[si]
        L = self.use_layout("dil", self._mk_dil)
        for (kp, rkp) in L["kp"].slots:
            S.op("dve", lambda v, kp=kp: v.memset(kp[:], 0.0), writes=[rkp])
        cnt = {"acc": 0, "sc": 0}
        units = []
        for g, dil in enumerate((1, 4, 16)):
            Lg = Sq // dil
            nb = Lg // 128
            for hh in range(4):
                head = g * 4 + hh
                hs = {}

                def loads(head=head, hs=hs, dil=dil, nb=nb, Lg=Lg):
                    qt_, rq = L["q"].next()
                    S.dma("sp", lambda q: q.dma_start(out=qt_[:, 0:Sq], in_=self.QKT[si][head]),
                          rq, reads=[self.r_QKT[si]], writes=[rq])
                    kp, rkp = L["kp"].next()
                    S.dma("sp", lambda q: q.dma_start(out=kp[:, 1024:1024 + Sq], in_=self.QKT[si][12 + head]),
                          rkp, reads=[self.r_QKT[si]], swrites=[rkp])
                    vp, rvp = L["vp"].next()
                    nblk = dil * (nb + 1)
                    vp4 = vp[:, 0:nblk, :].rearrange("p (r b) d -> p r b d", r=dil)
                    S.op("dve", lambda v: v.memset(vp[:, 0:nblk, :], 0.0), writes=[rvp])
                    S.op("dve", lambda v: v.memset(vp4[64:128, :, 0, 128:129], 1.0), swrites=[rvp])
                    if nb > 1:
                        S.op("dve", lambda v: v.memset(vp4[:, :, 1:nb, 128:129], 1.0), swrites=[rvp])
                    S.op("dve", lambda v: v.memset(vp4[0:64, :, nb, 128:129], 1.0), swrites=[rvp])
                    vs = self.VTM[si][:, head * 128:(head + 1) * 128].rearrange("(m r) d -> m r d", r=dil)
                    S.dma("sp", lambda q: q.dma_start(out=vp4[64:128, :, 0, 0:128], in_=vs[0:64]),
                          rvp, reads=[self.r_VTM[si]], swrites=[rvp])
                    if nb > 1:
                        for r in range(dil):
                            srcb = vs[64:64 + (nb - 1) * 128, r, :].rearrange("(b p) d -> p b d", p=128)
                            S.dma("sp", lambda q, srcb=srcb, r=r: q.dma_start(
                                out=vp4[:, r, 1:nb, 0:128], in_=srcb), rvp, reads=[self.r_VTM[si]], swrites=[rvp])
                    S.dma("sp", lambda q: q.dma_start(
                        out=vp4[0:64, :, nb, 0:128], in_=vs[Lg - 64:Lg]), rvp, reads=[self.r_VTM[si]], swrites=[rvp])
                    bd, rbd = L["bd"].next()
                    S.dma("sp", lambda q: q.dma_start(
                        out=bd[:], in_=self.w["c_bd"][:, head * 256:(head + 1) * 256]), rbd, writes=[rbd])
                    hs.update(q=(qt_, rq), kp=(kp, rkp), vp4=vp4, rvp=rvp, bd=(bd, rbd))

                pre_at = max(0, len(units) - 12)
                cnv = self.CN[si][g, :, hh, 0:129].rearrange("(m r) d -> m r d", r=dil)
                for r in range(dil):
                    for n in range(nb):
                        def mk(r=r, n=n, hs=hs, dil=dil, cnv=cnv, nb=nb):
                            hold = {}

                            def front():
                                qt_, rq = hs["q"]
                                kp, rkp = hs["kp"]
                                bd, rbd = hs["bd"]
                                sc, rsc = self.ps[cnt["sc"] % 2], self.r_ps[cnt["sc"] % 2]
                                cnt["sc"] += 1
                                q0 = 128 * n * dil + r
                                qsl = qt_[:, q0:q0 + 127 * dil + 1:dil] if dil > 1 else qt_[:, q0:q0 + 128]
                                for j in range(2):
                                    k0 = 1024 + (128 * (n + j) - 64) * dil + r
                                    ksl = kp[:, k0:k0 + 127 * dil + 1:dil] if dil > 1 else kp[:, k0:k0 + 128]
                                    S.op("pe", lambda t, ksl=ksl, j=j: t.matmul(
                                        sc[:, j * 128:(j + 1) * 128], lhsT=ksl, rhs=qsl, start=True, stop=True),
                                        reads=[rkp, rq], writes=[rsc], signal=(j == 1))
                                xt, rx = L["x"].next()
                                S.op("dve", lambda v: v.scalar_tensor_tensor(
                                    out=xt[:], in0=sc[:, 0:256], scalar=SCALE, in1=bd[:], op0=ALU.mult, op1=ALU.add),
                                    reads=[rsc, rbd], writes=[rx])
                                pt, rpt = L["pt"].next()
                                S.op("act", lambda a: a.activation(out=pt[:], in_=xt[:], func=AF.Exp),
                                     reads=[rx], writes=[rpt])
                                hold["pt"] = (pt, rpt)

                            def back():
                                pt, rpt = hold["pt"]
                                vp4, rvp = hs["vp4"], hs["rvp"]
                                ai = 2 + cnt["acc"] % 6
                                cnt["acc"] += 1
                                bk, rbk = self.ps[ai], self.r_ps[ai]
                                for j in range(2):
                                    S.op("pe", lambda t, j=j: t.matmul(
                                        bk[:, 0:129], lhsT=pt[:, j * 128:(j + 1) * 128], rhs=vp4[:, r, n + j, :],
                                        start=(j == 0), stop=(j == 1)), reads=[rpt, rvp], writes=[rbk], signal=(j == 1))
                                nbat = min(4, nb)
                                if n % nbat == 0:
                                    hs["cst"] = L["cst"].next()
                                cst, rcs = hs["cst"]
                                fst = (n % nbat == 0)
                                self.copy(self.ev_eng(), cst[:, n % nbat, 0:129], bk[:, 0:129], reads=[rbk],
                                          writes=([rcs] if fst else ()), swrites=(() if fst else [rcs]))
                                if n % nbat == nbat - 1:
                                    n0 = n - (nbat - 1)
                                    dst = cnv[128 * n0:128 * (n0 + nbat), r, :].rearrange("(b p) d -> p b d", p=128)
                                    S.dma("sp", lambda q: q.dma_start(out=dst, in_=cst[:, 0:nbat, 0:129]),
                                          rcs, reads=[rcs], swrites=[self.r_CN[si]])
                            return front, back
                        units.append(list(mk()) + [[]])
                units[pre_at][2].append(loads)
        self.run_units(units, 2)
        for blk in range(Sq // 128):
            cn, rcn = L["cnin"].next()
            src = self.CN[si][:, blk * 128:(blk + 1) * 128, :, :].rearrange("g p h d -> p g h d")
            S.dma("sp", lambda q, cn=cn, src=src: q.dma_start(out=cn[:], in_=src), rcn,
                  reads=[self.r_CN[si]], writes=[rcn])
            s1 = L["s1"]
            S.op("dve", lambda v, cn=cn: v.tensor_tensor(out=s1[:, :, 0:129], in0=cn[:, 0, :, 0:129],
                                                         in1=cn[:, 1, :, 0:129], op=ALU.add),
                 reads=[rcn], writes=[L["r_s1"]])
            S.op("dve", lambda v, cn=cn: v.tensor_tensor(out=s1[:, :, 0:129], in0=s1[:, :, 0:129],
                                                         in1=cn[:, 2, :, 0:129], op=ALU.add),
                 reads=[rcn, L["r_s1"]], writes=[L["r_s1"]])
            S.op("dve", lambda v: v.reciprocal(out=L["rden"][:].unsqueeze(2), in_=s1[:, :, 128:129]),
                 reads=[L["r_s1"]], writes=[L["r_rden"]])
            oc, roc = L["ocs"].next()
            S.op("dve", lambda v, oc=oc: v.tensor_tensor(
                out=oc[:], in0=s1[:, :, 0:128], in1=L["rden"][:].unsqueeze(2).broadcast_to([128, 4, 128]),
                op=ALU.mult), reads=[L["r_s1"], L["r_rden"]], writes=[roc])
            S.dma("sp", lambda q, oc=oc, blk=blk: q.dma_start(
                out=self.AO[si][blk * 128:(blk + 1) * 128, 0:512], in_=oc[:].rearrange("p h d -> p (h d)")),
                roc, reads=[roc], swrites=[self.r_AO[si]])

    def build(self):
        S = self.S
        self.load_consts()
        nseq = len(self.seqs)
        ph = self.dbg.get("phases", "abcd")
        if "i" not in ph:
            for si in range(nseq):
                self.phase_in(si)
        groups = [(layer, si) for layer in self.layers for si in range(nseq)]
        for gi_, (layer, si) in enumerate(groups):
            tiles = [self.tokbase[si] // TA + i for i in range(self.seqs[si] // TA)]
            if "a" in ph:
                for t in tiles:
                    self.phase_a(layer, si, t)
            if "b" in ph:
                if layer % 2 == 0:
                    if "1" not in ph:
                        self.dense_attn(si, "diff", layer)
                    if "2" not in ph:
                        self.na_attn(si)
                else:
                    if "1" not in ph:
                        self.dil_attn(si)
                    if "2" not in ph:
                        self.dense_attn(si, "gqa", layer)
            if "c" in ph:
                for t in tiles:
                    self.phase_c(layer, si, t)
            if "d" in ph:
                for k, t in enumerate(tiles):
                    nxt = None
                    if k + 1 < len(tiles):
                        nxt = ("d", layer, tiles[k + 1])
                    elif gi_ + 1 < len(groups) and "a" in ph and ph == "abcd":
                        nl, ns = groups[gi_ + 1]
                        nxt = ("a", nl, self.tokbase[ns] // TA)
                    self.phase_d(layer, t, nxt)
        if "o" not in ph:
            for si in range(nseq):
                self.phase_out(si)
        S.barrier([self.r_y], engines=("sp",))
        S.emit()
        return self.nc


_CACHE = {}


def _get_nc(seqs):
    key = tuple(seqs)
    if key not in _CACHE:
        _CACHE[key] = Builder(list(seqs)).build()
    return _CACHE[key]


def _c_ab():
    ab = np.zeros((128, 4, 32), np.float32)
    sl = _alibi(4)
    for h in range(4):
        ab[:, h, :] = -sl[h] * 128.0 * np.arange(32, dtype=np.float32)[None, :]
    return ab.reshape(128, 128)


def make_in_maps(inputs, cores, seq_names=("x_prompt", "x_sample")):
    f = lambda a: np.ascontiguousarray(np.asarray(a, dtype=np.float32))
    shared = {
        "attn_norm_g": f(inputs["attn_norm_g"]), "ev_w_in": f(inputs["ev_w_in"])[0],
        "ev_lambda": f(inputs["ev_lambda"])[0], "ev_subln_g": f(inputs["ev_subln_g"]),
        "rpbx": rpb_layout(inputs["ev_rpb"]).reshape(8, 128, 14 * 64),
        "ev_w_out": f(inputs["ev_w_out"])[0], "od_w_in": f(inputs["od_w_in"])[0],
        "od_qk_norm_g": f(inputs["od_qk_norm_g"])[0], "od_w_out": f(inputs["od_w_out"])[0],
        "ffn_norm_g": f(inputs["ffn_norm_g"]), "ffn_w_gate": f(inputs["ffn_w_gate"]),
        "ffn_w_up": f(inputs["ffn_w_up"]), "ffn_w_down": f(inputs["ffn_w_down"]),
        "final_norm_g": f(inputs["final_norm_g"]).reshape(1, D),
    }
    hc = host_consts()
    shared["c_ident"] = hc["c_ident"]
    shared["c_dq"] = hc["c_dq"].reshape(128, 5 * 512)
    shared["c_bd"] = hc["c_bd"].reshape(128, 12 * 256)
    shared["c_cm"] = hc["c_cm"]
    shared["c_cos"] = hc["c_cos"]
    shared["c_sin"] = hc["c_sin"]
    shared["c_ab"] = _c_ab()
    maps = []
    for b in cores:
        m = dict(shared)
        for i, n in enumerate(seq_names):
            m[f"x{i}"] = f(inputs[n][b])
        maps.append(m)
    return maps


def kernel(**inputs):
    n = 8
    seqs = (inputs["x_prompt"].shape[1], inputs["x_sample"].shape[1])
    nc = _get_nc(seqs)
    in_maps = make_in_maps(inputs, list(range(n)))
    res = run_bass_kernel_spmd(nc, in_maps, core_ids=list(range(n)))
    yp = np.stack([np.asarray(res.results[b]["y0"], dtype=np.float32) for b in range(n)], axis=0)
    ys = np.stack([np.asarray(res.results[b]["y1"], dtype=np.float32) for b in range(n)], axis=0)
    return (yp, ys)
```
